# Optimizing a Trainium2 kernel written in Bass

```python
import jax, jax.numpy as jnp
from jax import lax
import numpy as np

D_MODEL = 1024
BATCH = 16
SEQ = 2048
DEPTH = 2

HEAD_DIM = 64
N_HEADS = D_MODEL // HEAD_DIM
HEADS_A = (N_HEADS * 3) // 8
HEADS_B = N_HEADS // 4
HEADS_C = N_HEADS - HEADS_A - HEADS_B
D_A = HEADS_A * HEAD_DIM
D_B = HEADS_B * HEAD_DIM
D_C = HEADS_C * HEAD_DIM
D_MIX = D_A + D_B + D_C
D_PROJ = 2 * D_A + 2 * D_B + 3 * D_C
CONV_A_WIDTH = 31
CONV_C_WIDTH = 3
CHUNK = 128
D_FF = 128 * ((8 * D_MODEL // 3 + 127) // 128)
RMS_EPS = 1e-6
LN_EPS = 1e-5
SPLITS = [int(s) for s in np.cumsum([D_A, D_A, D_B, D_B, D_C, D_C, D_C])[:-1]]

kernel_name = "hybrid_conformer_gmlp_shortconv_macaron"


def rms_norm(x, g):
    xf = x.astype(jnp.float32)
    y = xf * lax.rsqrt(jnp.mean(xf * xf, axis=-1, keepdims=True) + RMS_EPS)
    return (y * g.astype(jnp.float32)).astype(x.dtype)


def layer_norm(x, g, b):
    xf = x.astype(jnp.float32)
    mu = jnp.mean(xf, axis=-1, keepdims=True)
    xc = xf - mu
    var = jnp.mean(xc * xc, axis=-1, keepdims=True)
    y = xc * lax.rsqrt(var + LN_EPS) * g.astype(jnp.float32) + b.astype(jnp.float32)
    return y.astype(x.dtype)


def causal_depthwise_conv(x, w):
    k, c = w.shape
    return lax.conv_general_dilated(
        x, w[:, None, :].astype(x.dtype), window_strides=(1,), padding=[(k - 1, 0)],
        dimension_numbers=("NWC", "WIO", "NWC"), feature_group_count=c)


def swiglu_ffn(h, w_gate, w_up, w_down):
    return (jax.nn.silu(h @ w_gate) * (h @ w_up)) @ w_down


def chunked_spatial_gating(u, v, ln_g, ln_b, w_s, b_s):
    bsz, seq, _ = v.shape
    v = layer_norm(v, ln_g, ln_b)
    v = v.reshape(bsz, seq // CHUNK, CHUNK, HEADS_B, HEAD_DIM)
    mask = jnp.tril(jnp.ones((CHUNK, CHUNK), dtype=bool))
    w = jnp.where(mask, w_s, 0).astype(v.dtype)
    s = jnp.einsum("hts,bcshd->bcthd", w, v) + b_s.T.astype(v.dtype)[None, None, :, :, None]
    return u * s.reshape(bsz, seq, D_B)


def setup_inputs(seed: int = 0) -> dict:
    key = jax.random.key(seed)
    ks = jax.random.split(key, 24)
    f32 = jnp.float32

    def nrm(k, shape, scale):
        return jax.random.normal(k, shape, f32) * scale

    def gain(k, shape):
        return 1.0 + 0.02 * jax.random.normal(k, shape, f32)

    L = DEPTH
    return {
        "x": jax.random.normal(ks[0], (BATCH, SEQ, D_MODEL), f32),
        "ffn1_norm": gain(ks[1], (L, D_MODEL)),
        "ffn1_w_gate": nrm(ks[2], (L, D_MODEL, D_FF), D_MODEL ** -0.5),
        "ffn1_w_up": nrm(ks[3], (L, D_MODEL, D_FF), D_MODEL ** -0.5),
        "ffn1_w_down": nrm(ks[4], (L, D_FF, D_MODEL), D_FF ** -0.5),
        "mix_norm": gain(ks[5], (L, D_MODEL)),
        "w_in": nrm(ks[6], (L, D_MODEL, D_PROJ), D_MODEL ** -0.5),
        "conv_a_w": nrm(ks[7], (L, CONV_A_WIDTH, D_A), CONV_A_WIDTH ** -0.5),
        "conv_a_b": nrm(ks[8], (L, D_A), 0.02),
        "ln_a_g": gain(ks[9], (L, D_A)),
        "ln_a_b": nrm(ks[10], (L, D_A), 0.02),
        "ln_v_g": gain(ks[11], (L, D_B)),
        "ln_v_b": nrm(ks[12], (L, D_B), 0.02),
        "w_s": nrm(ks[13], (L, HEADS_B, CHUNK, CHUNK), CHUNK ** -0.5),
        "b_s": gain(ks[14], (L, HEADS_B, CHUNK)),
        "conv_c_w": nrm(ks[15], (L, CONV_C_WIDTH, D_C), CONV_C_WIDTH ** -0.5),
        "w_out": nrm(ks[16], (L, D_MIX, D_MODEL), D_MIX ** -0.5),
        "ffn2_norm": gain(ks[17], (L, D_MODEL)),
        "ffn2_w_gate": nrm(ks[18], (L, D_MODEL, D_FF), D_MODEL ** -0.5),
        "ffn2_w_up": nrm(ks[19], (L, D_MODEL, D_FF), D_MODEL ** -0.5),
        "ffn2_w_down": nrm(ks[20], (L, D_FF, D_MODEL), D_FF ** -0.5),
        "final_norm": gain(ks[21], (D_MODEL,)),
    }


def reference(x, ffn1_norm, ffn1_w_gate, ffn1_w_up, ffn1_w_down, mix_norm, w_in,
              conv_a_w, conv_a_b, ln_a_g, ln_a_b, ln_v_g, ln_v_b, w_s, b_s, conv_c_w,
              w_out, ffn2_norm, ffn2_w_gate, ffn2_w_up, ffn2_w_down, final_norm):
    for l in range(DEPTH):
        h = rms_norm(x, ffn1_norm[l])
        x = x + 0.5 * swiglu_ffn(h, ffn1_w_gate[l], ffn1_w_up[l], ffn1_w_down[l])

        h = rms_norm(x, mix_norm[l])
        p = h @ w_in[l]
        a_val, a_gate, b_u, b_v, c_b, c_c, c_x = jnp.split(p, SPLITS, axis=-1)

        a = a_val * jax.nn.sigmoid(a_gate)
        a = causal_depthwise_conv(a, conv_a_w[l]) + conv_a_b[l]
        a = jax.nn.silu(layer_norm(a, ln_a_g[l], ln_a_b[l]))

        b = chunked_spatial_gating(jax.nn.gelu(b_u), jax.nn.gelu(b_v),
                                   ln_v_g[l], ln_v_b[l], w_s[l], b_s[l])

        c = c_b * causal_depthwise_conv(c_c * c_x, conv_c_w[l])

        x = x + jnp.concatenate([a, b, c], axis=-1) @ w_out[l]

        h = rms_norm(x, ffn2_norm[l])
        x = x + 0.5 * swiglu_ffn(h, ffn2_w_gate[l], ffn2_w_up[l], ffn2_w_down[l])
    return rms_norm(x, final_norm)
```

```python
import numpy as np
from contextlib import ExitStack

import concourse.bass as bass
import concourse.mybir as mybir
from concourse.bass_utils import run_bass_kernel_spmd

F32 = mybir.dt.float32
BF16 = mybir.dt.bfloat16
AF = mybir.ActivationFunctionType
ALU = mybir.AluOpType

D = 1024
DFF = 2816
NFC = DFF // 128
DPROJ = 2432
TILE = 512
RMS_EPS = 1e-6
LN_EPS = 1e-5
FFN_GROUPS = [4, 4, 4, 4, 3, 3]
PPL = 139
RING = 28672
SCR = 16960

PP_G1, PP_GM, PP_G2 = 0, 8, 16
PP_CAW, PP_CAB, PP_LAG, PP_LAB, PP_CCW, PP_BS = 24, 117, 120, 123, 126, 135


GRAN = 256
_ESZ = {}


def _phys(ap):
    try:
        if ap.tensor.name != "sb_scr":
            return ()
    except Exception:
        return ()
    esz = 2 if ap.dtype == BF16 else 4
    dims = list(ap.ap)
    pstep = int(dims[0][0]) if int(dims[0][0]) > 0 else 1 << 30
    off = int(ap.offset) % pstep
    ext = 1
    for st, cnt in dims[1:]:
        ext += (int(cnt) - 1) * abs(int(st))
    lo = off * esz
    hi = (off + ext) * esz
    return tuple(("g", i) for i in range(lo // GRAN, (hi - 1) // GRAN + 1))


def _is_ap(v):
    return hasattr(v, "tensor") and hasattr(v, "ap") and hasattr(v, "offset")


def _call_phys(name, a, k):
    pr, pw = [], []
    outs = []
    if "out" in k:
        outs.append(k["out"])
    elif a:
        outs.append(a[0])
    if k.get("accum_out") is not None:
        outs.append(k["accum_out"])
    for v in outs:
        if _is_ap(v):
            pw.extend(_phys(v))
    for i, v in enumerate(a):
        if i == 0 and "out" not in k:
            continue
        if _is_ap(v):
            pr.extend(_phys(v))
    for key, v in k.items():
        if key in ("out", "accum_out"):
            continue
        if _is_ap(v):
            pr.extend(_phys(v))
    return pr, pw


class _Rec:
    def __init__(self):
        self.call = None

    def __getattr__(self, name):
        def f(*a, **k):
            self.call = (name, a, k)
            return self
        return f


def _free_size(ap):
    n = 1
    for d in list(ap.shape)[1:]:
        n *= int(d)
    return n


def _est_dur(eng, name, a, k):
    out = k.get("out", a[0] if a else None)
    try:
        n = _free_size(out)
    except Exception:
        n = 512
    if eng == "pe":
        if name == "transpose":
            return 70.0
        rhs = k.get("rhs", a[2] if len(a) > 2 else None)
        try:
            n = _free_size(rhs)
        except Exception:
            pass
        return max(n, 64) * 0.42
    if eng == "act":
        return 230.0 + 0.75 * n
    if name == "reciprocal":
        return 60.0 + 2.7 * n
    return 60.0 + 1.2 * n


class Sched:
    def __init__(self, nc, sems, dsems):
        self.nc = nc
        self.e = {"pe": nc.tensor, "act": nc.scalar, "dve": nc.vector, "pool": nc.gpsimd, "sp": nc.sync}
        self.sem = sems
        self.dsem = dsems
        self.count = {k: 0 for k in ("pe", "act", "dve", "pool")}
        self.seen = {}
        self.lw = {}
        self.rd = {}
        self.win = None

    def _semobj(self, key):
        return self.sem[key] if key in self.sem else self.dsem[key][0]

    def _need(self, waiter, tok, waits):
        kind, key, idx = tok
        if kind == "e" and key != waiter:
            assert idx <= self.count[key], "wait on a pending (unsignaled) %s instruction from %s" % (key, waiter)
        if kind == "e" and key == waiter:
            if waiter == "pe":
                return
            if idx <= self.count[key] - 2:
                return
        if self.seen.get((waiter, key), 0) >= idx:
            return
        if waits.get(key, 0) < idx:
            waits[key] = idx

    def _collect(self, waiter, reads, writes):
        waits = {}
        for r in reads:
            t = self.lw.get(r)
            if t is not None:
                self._need(waiter, t, waits)
        for r in writes:
            t = self.lw.get(r)
            if t is not None:
                self._need(waiter, t, waits)
            for t in self.rd.get(r, {}).values():
                self._need(waiter, t, waits)
        return waits

    def _emit_waits(self, waiter, waits):
        E = self.e[waiter]
        for key, val in waits.items():
            E.wait_ge(self._semobj(key), val)
            self.seen[(waiter, key)] = val

    def _record(self, tok, reads, writes):
        for r in reads:
            self.rd.setdefault(r, {})[tok[1]] = tok
        for r in writes:
            self.lw[r] = tok
            self.rd[r] = {}

    def _op_now(self, eng, fn, reads=(), writes=(), signal=True):
        waits = self._collect(eng, reads, writes)
        self._emit_waits(eng, waits)
        ins = fn(self.e[eng])
        if signal:
            self.count[eng] += 1
            ins.then_inc(self.sem[eng], 1)
            self._record(("e", eng, self.count[eng]), reads, writes)
        else:
            self._record(("e", eng, self.count[eng] + 1), reads, writes)

    def _dma_now(self, q, pairs, semname, reads=(), writes=(), **kw):
        waits = self._collect(q, reads, writes)
        ds = self.dsem[semname]
        if ds[1] > self.seen.get((q, semname), 0):
            waits[semname] = max(waits.get(semname, 0), ds[1])
        self._emit_waits(q, waits)
        E = self.e[q]
        for out, in_ in pairs:
            E.dma_start(out=out, in_=in_, **kw).then_inc(ds[0], 16)
            ds[1] += 16
        self._record(("d", semname, ds[1]), reads, writes)

    def begin(self, tag=""):
        assert self.win is None
        import os
        if tag not in os.environ.get("KDEFER", "all").split(","):
            return
        self.win = []
        self.open_grp = {}

    def op(self, eng, fn, reads=(), writes=(), signal=True):
        rec = _Rec()
        fn(rec)
        name, a, k = rec.call
        pr, pw = _call_phys(name, a, k)
        reads = tuple(reads) + tuple(pr)
        writes = tuple(writes) + tuple(pw)
        if self.win is None:
            return self._op_now(eng, lambda e: getattr(e, name)(*a, **k), reads, writes, signal)
        dur = _est_dur(eng, name, a, k)
        g = self.open_grp.get(eng)
        if g is None:
            g = dict(kind="op", eng=eng, ins=[], reads=set(), writes=set(), dur=0.0)
            self.win.append(g)
            if not signal:
                self.open_grp[eng] = g
        g["ins"].append((name, a, k, reads, writes, signal))
        g["reads"].update(reads)
        g["writes"].update(writes)
        g["dur"] += dur
        if signal and eng in self.open_grp:
            del self.open_grp[eng]

    def dma(self, q, pairs, semname, reads=(), writes=(), **kw):
        reads = list(reads)
        writes = list(writes)
        for out, in_ in pairs:
            writes.extend(_phys(out))
            reads.extend(_phys(in_))
        if self.win is None:
            return self._dma_now(q, pairs, semname, reads, writes, **kw)
        self.win.append(dict(kind="dma", eng=q, pairs=pairs, semname=semname, reads=set(reads), writes=set(writes),
                             rl=tuple(reads), wl=tuple(writes), kw=kw, dur=2000.0))

    def flush(self):
        if self.win is None:
            return
        nodes = self.win
        assert not self.open_grp, "open unsignaled group at flush"
        self.win = None
        n = len(nodes)
        lastw = {}
        rds = {}
        preds = [set() for _ in range(n)]
        for i, nd in enumerate(nodes):
            for r in nd["reads"]:
                w = lastw.get(r)
                if w is not None and w != i:
                    preds[i].add(w)
            for r in nd["writes"]:
                w = lastw.get(r)
                if w is not None and w != i:
                    preds[i].add(w)
                for j in rds.get(r, ()):
                    if j != i:
                        preds[i].add(j)
            for r in nd["reads"]:
                rds.setdefault(r, set()).add(i)
            for r in nd["writes"]:
                lastw[r] = i
                rds[r] = set()
        succs = [[] for _ in range(n)]
        npred = [len(p) for p in preds]
        for i, p in enumerate(preds):
            for j in p:
                succs[j].append(i)
        import os
        LAT = float(os.environ.get("KLAT", "600"))
        BUCKET = float(os.environ.get("KBUCKET", "400"))
        tail = [0.0] * n
        for i in range(n - 1, -1, -1):
            m = 0.0
            for j in succs[i]:
                v = tail[j] + (0.0 if nodes[j]["eng"] == nodes[i]["eng"] else LAT)
                if v > m:
                    m = v
            tail[i] = nodes[i]["dur"] + m
        efree = {}
        fin = [0.0] * n
        rtime = [0.0] * n
        ready = [i for i in range(n) if npred[i] == 0]
        order = []
        while ready:
            best = None
            bkey = None
            for i in ready:
                st = max(efree.get(nodes[i]["eng"], 0.0), rtime[i])
                key = (int(st / BUCKET), -tail[i], i)
                if bkey is None or key < bkey:
                    bkey = key
                    best = i
            ready.remove(best)
            nd = nodes[best]
            st = max(efree.get(nd["eng"], 0.0), rtime[best])
            if os.environ.get("KCRIT"):
                if not hasattr(self, "_bind"):
                    self._bind = {}
                    self._elast = {}
                    self._rsrc = {}
                if rtime[best] >= efree.get(nd["eng"], 0.0):
                    self._bind[best] = ("dep", self._rsrc.get(best))
                else:
                    self._bind[best] = ("eng", self._elast.get(nd["eng"]))
                self._elast[nd["eng"]] = best
            fin[best] = st + nd["dur"]
            efree[nd["eng"]] = fin[best] if nd["kind"] == "op" else st + 100.0
            order.append(best)
            for j in succs[best]:
                lat = 0.0 if nodes[j]["eng"] == nd["eng"] else LAT
                if fin[best] + lat > rtime[j]:
                    rtime[j] = fin[best] + lat
                    if os.environ.get("KCRIT"):
                        self._rsrc[j] = best
                npred[j] -= 1
                if npred[j] == 0:
                    ready.append(j)
        assert len(order) == n
        import os
        if os.environ.get("KSCHED", "1") == "0":
            order = list(range(n))
        self.est_time = getattr(self, "est_time", 0.0) + (max(fin) if n else 0.0)
        if os.environ.get("KCRIT"):
            tq = float(os.environ["KCRIT"]) * 1000.0
            cand = [i for i in range(n) if nodes[i]["eng"] == "pe" and fin[i] <= tq]
            cur = max(cand, key=lambda i: fin[i])
            for _ in range(int(os.environ.get("KCRITN", "60"))):
                nd = nodes[cur]
                nm = nd["ins"][0][0] if nd["kind"] == "op" else "dma"
                kind, prev = self._bind.get(cur, (None, None))
                w = sorted(str(r) for r in nd["writes"] if r[0] != "g")[:2]
                print("%8.1f-%8.1f %-4s %-22s n=%d W=%s  <- %s" % ((fin[cur] - nd["dur"]) / 1e3, fin[cur] / 1e3, nd["eng"], nm, len(nd.get("ins", [])), w, kind))
                if prev is None:
                    break
                cur = prev
        if os.environ.get("KVERB"):
            busy = {}
            for i in range(n):
                busy[nodes[i]["eng"]] = busy.get(nodes[i]["eng"], 0.0) + nodes[i]["dur"]
            B = 50000.0
            hist = {}
            for i in range(n):
                if nodes[i]["eng"] == "pe":
                    st_ = fin[i] - nodes[i]["dur"]
                    b0 = int(st_ // B)
                    hist[b0] = hist.get(b0, 0.0) + nodes[i]["dur"]
            print("PE util per 50us:", " ".join("%d" % round(100 * hist.get(b, 0.0) / B) for b in range(int(max(fin) // B) + 1)))
            print("sched: nodes", n, "est makespan us %.1f" % (max(fin) / 1e3), {k: round(v / 1e3, 1) for k, v in busy.items()})
        if os.environ.get("KDUMP"):
            for pos, i in enumerate(order[:int(os.environ["KDUMP"])]):
                nd = nodes[i]
                nm = nd["ins"][0][0] if nd["kind"] == "op" else "dma"
                print(pos, i, nd["eng"], nm, len(nd.get("ins", [])), "W", sorted(map(str, nd["writes"]))[:3], "R", sorted(map(str, nd["reads"]))[:4], "t=%.0f" % fin[i])
        for i in order:
            nd = nodes[i]
            if nd["kind"] == "dma":
                self._dma_now(nd["eng"], nd["pairs"], nd["semname"], nd["rl"], nd["wl"], **nd["kw"])
            else:
                for (name, a, k, rl, wl, sig) in nd["ins"]:
                    self._op_now(nd["eng"], lambda e: getattr(e, name)(*a, **k), rl, wl, sig)

    def barrier_q(self, q):
        assert self.win is None
        waits = {}
        for p in ("pe", "act", "dve"):
            if self.count[p] > self.seen.get((q, p), 0):
                waits[p] = self.count[p]
        self._emit_waits(q, waits)

    def barrier(self):
        assert self.win is None
        engs = ("pe", "act", "dve")
        tgt = dict(self.count)
        for e in engs:
            waits = {}
            for p in engs:
                if tgt[p] > self.seen.get((e, p), 0):
                    waits[p] = tgt[p]
            for k, ds in self.dsem.items():
                if k.startswith("io") and ds[1] > self.seen.get((e, k), 0):
                    waits[k] = ds[1]
            self._emit_waits(e, waits)


def build_program(nseq, seq, nlayers, do_final=True, ring=RING):
    NT = seq // TILE
    nc = bass.Bass("TRN2", target_bir_lowering=False)
    L = nlayers

    def din(name, shape):
        return nc.dram_tensor(name, list(shape), F32, kind="ExternalInput").ap()

    x_d = din("x", [nseq, seq, D])
    wd = {}
    for f in ("ffn1", "ffn2"):
        wd[f + "_w_gate"] = din(f + "_w_gate", [L, D, DFF])
        wd[f + "_w_up"] = din(f + "_w_up", [L, D, DFF])
        wd[f + "_w_down"] = din(f + "_w_down", [L, DFF, D])
    wd["w_in"] = din("w_in", [L, D, DPROJ])
    wd["w_out"] = din("w_out", [L, D, D])
    pp_d = din("pp", [128, L * PPL])
    bc_d = din("bc", [128, L * 512 + 1024])
    cst_d = din("cst", [128, 128 + 512])
    ws_d = din("wsT", [128, L * 512])
    y_d = nc.dram_tensor("y", [nseq, seq, D], F32, kind="ExternalOutput").ap()

    with ExitStack() as es:
        def sb(name, shape, dt):
            return es.enter_context(nc.sbuf_tensor("sb_" + name, list(shape), dt))

        xT = sb("xT", [128, 8, seq], F32)
        scr = sb("scr", [128, SCR], F32)
        wring = sb("wring", [128, ring], BF16)
        sq = sb("sq", [128, 6, TILE], BF16)
        stdb = sb("stdb", [128, 2, TILE], F32)
        sg = sb("sg", [128, 2, TILE], F32)
        ident = sb("ident", [128, 128], F32)
        ones_d = sb("ones_d", [128, 128], BF16)
        ones_a = sb("ones_a", [128, 128], BF16)
        identb = sb("identb", [128, 128], BF16)
        pp = sb("pp", [128, L * PPL], F32)
        bc = sb("bc", [128, 512], F32)
        wsT = sb("wsT", [128, L * 512], BF16)
        epsr = sb("epsr", [128, 1], F32)
        epsl = sb("epsl", [128, 1], F32)
        small = sb("small", [128, 32], F32)
        ps = es.enter_context(nc.psum_tensor("ps", [128, 8, TILE], F32))

        sems = {k: es.enter_context(nc.semaphore("s_" + k)) for k in ("pe", "act", "dve", "pool")}
        dnames = ["w%d" % i for i in range(8)] + ["io%d" % i for i in range(4)] + ["c0", "c1"]
        dsems = {k: [es.enter_context(nc.semaphore("d_" + k)), 0] for k in dnames}
        S = Sched(nc, sems, dsems)

        def carve_bf(off_words, nwords):
            return scr[:, off_words:off_words + nwords].bitcast(BF16)

        h_all = carve_bf(0, NT * 8 * TILE // 2).rearrange("p (t c n) -> p t c n", t=NT, c=8)
        act_b = carve_bf(11648, 2 * 4 * TILE // 2).rearrange("p (a f n) -> p a f n", a=2, f=4)
        hm2 = carve_bf(0, 4096).rearrange("p (b c n) -> p b c n", b=2, c=8)
        a_bf = carve_bf(4096, 816).rearrange("p (c n) -> p c n", c=3)
        m_buf = scr[:, 4928:4928 + 3 * 516].rearrange("p (c n) -> p c n", c=3)
        cv = scr[:, 6528:6528 + TILE].rearrange("p (c n) -> p c n", c=1)
        guv = scr[:, 7040:7040 + 2 * TILE].rearrange("p (c n) -> p c n", c=2)
        vn = scr[:, 8064:8064 + 512].rearrange("p (c n) -> p c n", c=2)
        vnb = carve_bf(8576, 256).rearrange("p (c n) -> p c n", c=2)
        btm = carve_bf(8832, 256).rearrange("p (c n) -> p c n", c=2)
        acc = scr[:, 9088:9088 + 3 * TILE].rearrange("p (c n) -> p c n", c=3)
        lnm = scr[:, 10624:10624 + TILE]
        lnr = scr[:, 11136:11136 + TILE]
        mix2 = carve_bf(11648, 4096).rearrange("p (b c n) -> p b c n", b=2, c=8)
        NDG = 16
        diag = carve_bf(15744, 1024).rearrange("p (c n) -> p c n", c=NDG)
        bnst = scr[:, 16768:16784].rearrange("p (c n) -> p c n", c=2)
        mvb = scr[:, 16832:16840].rearrange("p (c n) -> p c n", c=2)
        assert SCR >= 16896
        junk = scr[:, 0:D]
        gfin = scr[:, D:2 * D]
        wstage = scr[:, 2 * D:2 * D + L * 512]
        mask4 = scr[:, 2 * D + L * 512:2 * D + L * 512 + 512]
        NSTG = 8
        stg = scr[:, 4096:4096 + NSTG * D].rearrange("p (a n) -> p a n", a=NSTG)

        bank_ctr = [0]
        reserved = set()

        def bank():
            while True:
                b = bank_ctr[0] % 8
                bank_ctr[0] += 1
                if b not in reserved:
                    return b

        def bank_pair():
            if bank_ctr[0] % 2:
                bank_ctr[0] += 1
            b = bank_ctr[0] % 8
            bank_ctr[0] += 2
            return b

        units = []
        ring_state = {"head": 0, "live": [], "next": 0, "alloc": {}, "dead": []}

        def ring_try_prefetch():
            st = ring_state
            while st["next"] < len(units):
                u = units[st["next"]]
                size = u["size"]
                off = u["off"]
                assert off + size <= ring
                ok = True
                for (uid, lo, hi) in st["live"]:
                    if lo < off + size and off < hi:
                        ok = False
                        break
                if not ok:
                    return
                wr = [("w", u["uid"])]
                keep = []
                for (uid, lo, hi) in st["dead"]:
                    if lo < off + size and off < hi:
                        if ("w", uid) not in wr:
                            wr.append(("w", uid))
                        if lo < off:
                            keep.append((uid, lo, off))
                        if hi > off + size:
                            keep.append((uid, off + size, hi))
                    else:
                        keep.append((uid, lo, hi))
                st["dead"] = keep
                pairs = [(mk(off), src) for (mk, src) in u["dmas"]]
                S.dma("pool", pairs, "w%d" % (u["uid"] % 8), reads=(), writes=wr)
                st["alloc"][u["uid"]] = off
                st["live"].append((u["uid"], off, off + size))
                st["head"] = off + size
                st["next"] += 1

        def ring_release(uid):
            st = ring_state
            for i, (u, lo, hi) in enumerate(st["live"]):
                if u == uid:
                    st["dead"].append(st["live"].pop(i))
                    break
            ring_try_prefetch()

        def unit_off(uid):
            assert uid in ring_state["alloc"], "unit %d not loaded" % uid
            return ring_state["alloc"][uid]

        def add_ffn_units(pref, l):
            ids = []
            f0 = 0
            for G in FFN_GROUPS:
                uid = len(units)
                gcols = G * 128

                def mk_gu(which, G=G, gcols=gcols):
                    def mk(off):
                        v = wring[:, off:off + 8 * 2 * gcols].rearrange("p (k w n) -> p k w n", k=8, w=2)
                        return v[:, :, which, :]
                    return mk

                def mk_dn(G=G, gcols=gcols):
                    def mk(off):
                        o2 = off + 8 * 2 * gcols
                        return wring[:, o2:o2 + G * D].rearrange("p (f n) -> p f n", f=G)
                    return mk

                src_g = wd[pref + "_w_gate"][l, :, f0 * 128:(f0 + G) * 128].rearrange("(k p) n -> p k n", p=128)
                src_u = wd[pref + "_w_up"][l, :, f0 * 128:(f0 + G) * 128].rearrange("(k p) n -> p k n", p=128)
                src_d = wd[pref + "_w_down"][l, f0 * 128:(f0 + G) * 128, :].rearrange("(f p) n -> p f n", p=128)
                units.append(dict(uid=uid, size=8 * 2 * gcols + G * D, off=(12288 if len(ids) % 2 == 0 else 0),
                                  dmas=[(mk_gu(0), src_g), (mk_gu(1), src_u), (mk_dn(), src_d)]))
                ids.append((uid, G, f0))
                f0 += G
            return ids

        def add_mixer_units(l):
            ids = {}
            for name, c0, c1 in (("A", 0, 768), ("B", 768, 1280), ("C", 1280, 2432)):
                uid = len(units)
                ncol = c1 - c0

                def mk(off, ncol=ncol):
                    return wring[:, off:off + 8 * ncol].rearrange("p (k n) -> p k n", k=8)

                src = wd["w_in"][l, :, c0:c1].rearrange("(k p) n -> p k n", p=128)
                units.append(dict(uid=uid, size=8 * ncol, off={"A": 9216, "B": 15360, "C": 19456}[name], dmas=[(mk, src)]))
                ids[name] = (uid, ncol)
            uid = len(units)

            def mko(off):
                return wring[:, off:off + 8 * D].rearrange("p (k n) -> p k n", k=8)

            src = wd["w_out"][l].rearrange("(k p) n -> p k n", p=128)
            units.append(dict(uid=uid, size=8 * D, off=0, dmas=[(mko, src)]))
            ids["O"] = (uid, D)
            return ids

        plan = []
        for s in range(nseq):
            for l in range(L):
                plan.append((s, l, add_ffn_units("ffn1", l), add_mixer_units(l), add_ffn_units("ffn2", l)))

        S.dma("sp", [(pp[:], pp_d)], "c0", writes=[("pp",)])
        S.dma("sp", [(ident[:], cst_d[:, 0:128]), (mask4, cst_d[:, 128:640]), (wstage, ws_d)], "c0",
              writes=[("ident",), ("mask4",), ("wstage",)])
        S.op("dve", lambda e: e.memset(ones_d[:], 1.0 / 1024.0), writes=[("ones_d",)])
        S.op("dve", lambda e: e.memset(ones_a[:], 1.0 / 384.0), writes=[("ones_a",)])
        S.op("dve", lambda e: e.memset(epsr[:], RMS_EPS), writes=[("epsr",)])
        S.op("dve", lambda e: e.memset(epsl[:], LN_EPS), writes=[("epsl",)])
        S.op("dve", lambda e: e.tensor_copy(out=identb[:], in_=ident[:]), reads=[("ident",)], writes=[("identb",)])
        for l in range(L):
            S.op("dve", lambda e, l=l: e.tensor_tensor(out=wsT[:, l * 512:(l + 1) * 512], in0=wstage[:, l * 512:(l + 1) * 512],
                                                       in1=mask4, op=ALU.mult),
                 reads=[("wstage",), ("mask4",)], writes=[("wsT", l)])
        S.barrier()
        ring_try_prefetch()

        def xreg(t, c):
            return ("x", t, c)

        def rms_norm_tile(l, gcol, t, hdst, hreg):
            tok = slice(t * TILE, (t + 1) * TILE)
            b = bank()
            for c in range(8):
                slot = c % 6
                S.op("act", lambda e, c=c, slot=slot: e.activation(out=sq[:, slot, :], in_=xT[:, c, tok], func=AF.Square),
                     reads=[xreg(t, c)], writes=[("sq", slot)])
                S.op("pe", lambda e, c=c, slot=slot: e.matmul(ps[:, b, :], ones_d[:], sq[:, slot, :], start=(c == 0), stop=(c == 7)),
                     reads=[("sq", slot), ("ones_d",)], writes=[("ps", b)])
            S.op("act", lambda e: e.activation(out=stdb[:, 0, :], in_=ps[:, b, :], func=AF.Sqrt, bias=epsr[:, 0:1], scale=1.0),
                 reads=[("ps", b), ("epsr",)], writes=[("std",)])
            S.op("dve", lambda e: e.reciprocal(out=stdb[:, 1, :], in_=stdb[:, 0, :]), reads=[("std",)], writes=[("rstd",)])
            for c in range(8):
                S.op("dve", lambda e, c=c: e.scalar_tensor_tensor(out=hdst[:, c, :], in0=xT[:, c, tok],
                                                                  scalar=pp[:, l * PPL + gcol + c:l * PPL + gcol + c + 1],
                                                                  in1=stdb[:, 1, :], op0=ALU.mult, op1=ALU.mult),
                     reads=[xreg(t, c), ("rstd",), ("pp",)], writes=[hreg(c)])

        sg_ctr = [0]

        def ffn_phase(l, gcol, unit_ids):
            steps = [(gi, t) for gi in range(len(unit_ids)) for t in range(NT)]

            def GU(i):
                gi, t = steps[i]
                uid, G, f0 = unit_ids[gi]
                off = unit_off(uid)
                gu = wring[:, off:off + 8 * 2 * G * 128].rearrange("p (k w n) -> p k w n", k=8, w=2)
                ab = i % 2
                for fi in range(G):
                    bA = bank()
                    for k in range(8):
                        S.op("pe", lambda e, k=k: e.matmul(ps[:, bA, :], gu[:, k, 0, fi * 128:(fi + 1) * 128], h_all[:, t, k, :],
                                                           start=(k == 0), stop=(k == 7)),
                             reads=[("w", uid), ("h", t, k)], writes=[("ps", bA)], signal=(k == 7))
                    bB = bank()
                    for k in range(8):
                        S.op("pe", lambda e, k=k: e.matmul(ps[:, bB, :], gu[:, k, 1, fi * 128:(fi + 1) * 128], h_all[:, t, k, :],
                                                           start=(k == 0), stop=(k == 7)),
                             reads=[("w", uid), ("h", t, k)], writes=[("ps", bB)], signal=(k == 7))
                    sl = sg_ctr[0] % 2
                    sg_ctr[0] += 1
                    S.op("act", lambda e: e.activation(out=sg[:, sl, :], in_=ps[:, bA, :], func=AF.Silu),
                         reads=[("ps", bA)], writes=[("sg", sl)])
                    S.op("dve", lambda e: e.tensor_tensor(out=act_b[:, ab, fi, :], in0=sg[:, sl, :], in1=ps[:, bB, :], op=ALU.mult),
                         reads=[("sg", sl), ("ps", bB)], writes=[("act", ab, fi)])

            def DN(i):
                gi, t = steps[i]
                uid, G, f0 = unit_ids[gi]
                off = unit_off(uid)
                dn = wring[:, off + 8 * 2 * G * 128:off + 8 * 2 * G * 128 + G * D].rearrange("p (f n) -> p f n", f=G)
                ab = i % 2
                tok = slice(t * TILE, (t + 1) * TILE)
                for dc in range(8):
                    b = bank()
                    for fi in range(G):
                        S.op("pe", lambda e, fi=fi: e.matmul(ps[:, b, :], dn[:, fi, dc * 128:(dc + 1) * 128], act_b[:, ab, fi, :],
                                                             start=(fi == 0), stop=(fi == G - 1)),
                             reads=[("w", uid), ("act", ab, fi)], writes=[("ps", b)], signal=(fi == G - 1))
                    S.op("dve", lambda e: e.scalar_tensor_tensor(out=xT[:, dc, tok], in0=ps[:, b, :], scalar=0.5, in1=xT[:, dc, tok],
                                                                 op0=ALU.mult, op1=ALU.add),
                         reads=[("ps", b), xreg(t, dc)], writes=[xreg(t, dc)])
                if t == NT - 1:
                    ring_release(uid)

            for i in range(len(steps)):
                gi, t = steps[i]
                if gi == 0:
                    rms_norm_tile(l, gcol, t, h_all[:, t], lambda c, t=t: ("h", t, c))
                GU(i)
                if i > 0:
                    DN(i - 1)
            DN(len(steps) - 1)

        def mixer_phase(l, mu):
            uA, _ = mu["A"]
            uB, _ = mu["B"]
            uC, _ = mu["C"]
            uO, _ = mu["O"]
            P0 = l * PPL
            dg_ctr = [0]
            S.dma("sp", [(bc[:], bc_d[:, l * 512:(l + 1) * 512])], "c1", writes=[("bc",)])

            def m_norm(t):
                rms_norm_tile(l, PP_GM, t, hm2[:, t % 2], lambda c, t=t: ("hm", t % 2, c))

            def m_inproj(t):
                mix = mix2[:, t % 2]
                mb = t % 2
                hm = hm2[:, t % 2]
                wA = wring[:, unit_off(uA):unit_off(uA) + 8 * 768].rearrange("p (k n) -> p k n", k=8)
                wB = wring[:, unit_off(uB):unit_off(uB) + 8 * 512].rearrange("p (k n) -> p k n", k=8)
                wC = wring[:, unit_off(uC):unit_off(uC) + 8 * 1152].rearrange("p (k n) -> p k n", k=8)
                if t == 0:
                    S.op("dve", lambda e: e.memset(a_bf[:, :, 0:32], 0.0), writes=[("a_halo",)])
                    S.op("dve", lambda e: e.memset(m_buf[:, :, 0:4], 0.0), writes=[("m_halo",)])
                else:
                    S.op("act", lambda e: e.activation(out=a_bf[:, :, 0:32], in_=a_bf[:, :, 512:544], func=AF.Copy),
                         reads=[("a", 0), ("a", 1), ("a", 2)], writes=[("a_halo",)])
                    S.op("act", lambda e: e.activation(out=m_buf[:, :, 0:4], in_=m_buf[:, :, 512:516], func=AF.Copy),
                         reads=[("m", 0), ("m", 1), ("m", 2)], writes=[("m_halo",)])

                def proj(wv, c0, b, uid):
                    for k in range(8):
                        S.op("pe", lambda e, k=k: e.matmul(ps[:, b, :], wv[:, k, c0:c0 + 128], hm[:, k, :], start=(k == 0), stop=(k == 7)),
                             reads=[("w", uid), ("hm", mb, k)], writes=[("ps", b)], signal=(k == 7))

                for c in range(3):
                    bv = bank()
                    proj(wA, c * 128, bv, uA)
                    bg = bank()
                    proj(wA, 384 + c * 128, bg, uA)
                    sl = sg_ctr[0] % 2
                    sg_ctr[0] += 1
                    S.op("act", lambda e: e.activation(out=sg[:, sl, :], in_=ps[:, bg, :], func=AF.Sigmoid),
                         reads=[("ps", bg)], writes=[("sg", sl)])
                    S.op("dve", lambda e, c=c: e.tensor_tensor(out=a_bf[:, c, 32:544], in0=sg[:, sl, :], in1=ps[:, bv, :], op=ALU.mult),
                         reads=[("sg", sl), ("ps", bv)], writes=[("a", c)])
                for c in range(3):
                    bb_ = bank()
                    proj(wC, c * 128, bb_, uC)
                    bcc = bank()
                    proj(wC, 384 + c * 128, bcc, uC)
                    bx = bank()
                    proj(wC, 768 + c * 128, bx, uC)
                    sl = sg_ctr[0] % 2
                    sg_ctr[0] += 1
                    S.op("act", lambda e: e.activation(out=sg[:, sl, :], in_=ps[:, bx, :], func=AF.Copy),
                         reads=[("ps", bx)], writes=[("sg", sl)])
                    S.op("dve", lambda e, c=c: e.tensor_tensor(out=m_buf[:, c, 4:516], in0=sg[:, sl, :], in1=ps[:, bcc, :], op=ALU.mult),
                         reads=[("sg", sl), ("ps", bcc)], writes=[("m", c)])
                    cs = 0
                    wcol = P0 + PP_CCW + c * 3
                    S.op("dve", lambda e, c=c: e.tensor_scalar(out=cv[:, cs, :], in0=m_buf[:, c, 2:514], scalar1=pp[:, wcol:wcol + 1], scalar2=None,
                                                               op0=ALU.mult),
                         reads=[("m", c), ("m_halo",), ("pp",)], writes=[("cv", cs)])
                    for k in (1, 2):
                        S.op("dve", lambda e, c=c, k=k: e.scalar_tensor_tensor(out=cv[:, cs, :], in0=m_buf[:, c, 2 + k:514 + k],
                                                                               scalar=pp[:, wcol + k:wcol + k + 1], in1=cv[:, cs, :],
                                                                               op0=ALU.mult, op1=ALU.add),
                             reads=[("m", c), ("m_halo",), ("cv", cs), ("pp",)], writes=[("cv", cs)])
                    S.op("dve", lambda e, c=c: e.tensor_tensor(out=mix[:, 5 + c, :], in0=cv[:, cs, :], in1=ps[:, bb_, :], op=ALU.mult),
                         reads=[("cv", cs), ("ps", bb_)], writes=[("mix", mb, 5 + c)])
                bT = bank()
                reserved.add(bT)
                psb = ps[:, bT, :].bitcast(BF16)
                for j in range(4):
                    buv = bank()
                    for k in range(8):
                        S.op("pe", lambda e, k=k: e.matmul(ps[:, buv, :], hm[:, k, j * 128:(j + 1) * 128], wB[:, k, :], start=(k == 0), stop=(k == 7)),
                             reads=[("w", uB), ("hm", mb, k)], writes=[("ps", buv)], signal=(k == 7))
                    gs = j % 2
                    S.op("act", lambda e: e.activation(out=guv[:, gs, :], in_=ps[:, buv, :], func=AF.Gelu_apprx_tanh),
                         reads=[("ps", buv)], writes=[("guv", gs)])
                    S.op("dve", lambda e: e.bn_stats(out=bnst[:, gs, 0:6], in_=guv[:, gs, 256:512]), reads=[("guv", gs)], writes=[("bnst", gs)])
                    S.op("dve", lambda e: e.bn_aggr(out=mvb[:, gs, 0:2], in_=bnst[:, gs, 0:6]), reads=[("bnst", gs)], writes=[("mv", gs)])
                    S.op("act", lambda e: e.activation(out=mvb[:, gs, 2:3], in_=mvb[:, gs, 1:2], func=AF.Sqrt, bias=epsl[:, 0:1], scale=1.0),
                         reads=[("mv", gs), ("epsl",)], writes=[("sdv", gs)])
                    S.op("dve", lambda e: e.reciprocal(out=mvb[:, gs, 3:4], in_=mvb[:, gs, 2:3]), reads=[("sdv", gs)], writes=[("rsv", gs)])
                    S.op("dve", lambda e: e.tensor_scalar(out=vn[:, gs, :], in0=guv[:, gs, 256:512], scalar1=mvb[:, gs, 0:1], scalar2=mvb[:, gs, 3:4],
                                                          op0=ALU.subtract, op1=ALU.mult),
                         reads=[("guv", gs), ("mv", gs), ("rsv", gs)], writes=[("vn", gs)])
                    S.op("dve", lambda e: e.tensor_tensor(out=vn[:, gs, :], in0=vn[:, gs, :], in1=bc[:, 0:256], op=ALU.mult),
                         reads=[("vn", gs), ("bc",)], writes=[("vn", gs)])
                    S.op("dve", lambda e: e.tensor_tensor(out=vnb[:, gs, :], in0=vn[:, gs, :], in1=bc[:, 256:512], op=ALU.add),
                         reads=[("vn", gs), ("bc",)], writes=[("vnb", gs)])
                    bs_ = bank()
                    for hh in range(4):
                        S.op("pe", lambda e, hh=hh: e.matmul(ps[:, bs_, hh * 64:(hh + 1) * 64], wsT[:, l * 512 + hh * 128:l * 512 + (hh + 1) * 128],
                                                             vnb[:, gs, hh * 64:(hh + 1) * 64], start=True, stop=True),
                             reads=[("wsT", l), ("vnb", gs)], writes=[("ps", bs_)])
                    for hh in range(4):
                        S.op("dve", lambda e, hh=hh: e.scalar_tensor_tensor(out=btm[:, gs, hh * 64:(hh + 1) * 64], in0=ps[:, bs_, hh * 64:(hh + 1) * 64],
                                                                            scalar=pp[:, P0 + PP_BS + hh:P0 + PP_BS + hh + 1],
                                                                            in1=guv[:, gs, hh * 64:(hh + 1) * 64], op0=ALU.add, op1=ALU.mult),
                             reads=[("ps", bs_), ("guv", gs), ("pp",)], writes=[("btm", gs)])
                    for cc in range(2):
                        S.op("pe", lambda e, cc=cc: e.transpose(psb[:, cc * 512 + j * 128:cc * 512 + (j + 1) * 128], btm[:, gs, cc * 128:(cc + 1) * 128], identb[:]),
                             reads=[("btm", gs), ("identb",)], writes=[("ps", bT)])
                for cc in range(2):
                    S.op("act", lambda e, cc=cc: e.activation(out=mix[:, 3 + cc, :], in_=psb[:, cc * 512:(cc + 1) * 512], func=AF.Copy),
                         reads=[("ps", bT)], writes=[("mix", mb, 3 + cc)])
                reserved.discard(bT)
                if t == NT - 1:
                    ring_release(uA)
                    ring_release(uB)
                    ring_release(uC)

            def m_conv(t):
                for c in range(3):
                    b = bank()
                    for k in range(31):
                        ds_ = dg_ctr[0] % NDG
                        dg_ctr[0] += 1
                        wcol = P0 + PP_CAW + c * 31 + k
                        S.op("dve", lambda e: e.tensor_scalar(out=diag[:, ds_, :], in0=identb[:], scalar1=pp[:, wcol:wcol + 1], scalar2=None, op0=ALU.mult),
                             reads=[("identb",), ("pp",)], writes=[("diag", ds_)])
                        S.op("pe", lambda e: e.matmul(ps[:, b, :], diag[:, ds_, :], a_bf[:, c, 2 + k:514 + k], start=(k == 0), stop=(k == 30)),
                             reads=[("diag", ds_), ("a", c), ("a_halo",)], writes=[("ps", b)])
                    bcol = P0 + PP_CAB + c
                    S.op("dve", lambda e: e.tensor_scalar(out=acc[:, c, :], in0=ps[:, b, :], scalar1=pp[:, bcol:bcol + 1], scalar2=None, op0=ALU.add),
                         reads=[("ps", b), ("pp",)], writes=[("acc", c)])
                    S.op("act", lambda e: e.activation(out=sq[:, c, :], in_=acc[:, c, :], func=AF.Copy), reads=[("acc", c)], writes=[("sq", c)])
                    S.op("act", lambda e: e.activation(out=sq[:, 3 + c, :], in_=acc[:, c, :], func=AF.Square), reads=[("acc", c)], writes=[("sq", 3 + c)])
                bm = bank()
                be = bank()
                for c in range(3):
                    S.op("pe", lambda e, c=c: e.matmul(ps[:, bm, :], ones_a[:], sq[:, c, :], start=(c == 0), stop=(c == 2)),
                         reads=[("sq", c), ("ones_a",)], writes=[("ps", bm)])
                for c in range(3):
                    S.op("pe", lambda e, c=c: e.matmul(ps[:, be, :], ones_a[:], sq[:, 3 + c, :], start=(c == 0), stop=(c == 2)),
                         reads=[("sq", 3 + c), ("ones_a",)], writes=[("ps", be)])
                return bm, be

            def m_lnapply(t, bm, be):
                mix = mix2[:, t % 2]
                mb = t % 2
                S.op("act", lambda e: e.activation(out=lnm, in_=ps[:, bm, :], func=AF.Copy), reads=[("ps", bm)], writes=[("lnm",)])
                S.op("dve", lambda e: e.tensor_tensor(out=lnr, in0=lnm, in1=lnm, op=ALU.mult), reads=[("lnm",)], writes=[("lnr",)])
                S.op("dve", lambda e: e.tensor_tensor(out=lnr, in0=ps[:, be, :], in1=lnr, op=ALU.subtract),
                     reads=[("ps", be), ("lnr",)], writes=[("lnr",)])
                S.op("dve", lambda e: e.tensor_scalar(out=lnr, in0=lnr, scalar1=0.0, scalar2=None, op0=ALU.max),
                     reads=[("lnr",)], writes=[("lnr",)])
                S.op("act", lambda e: e.activation(out=lnr, in_=lnr, func=AF.Sqrt, bias=epsl[:, 0:1], scale=1.0),
                     reads=[("lnr",), ("epsl",)], writes=[("lnr",)])
                S.op("dve", lambda e: e.reciprocal(out=lnr, in_=lnr), reads=[("lnr",)], writes=[("lnr",)])
                for c in range(3):
                    S.op("dve", lambda e, c=c: e.tensor_tensor(out=acc[:, c, :], in0=acc[:, c, :], in1=lnm, op=ALU.subtract),
                         reads=[("acc", c), ("lnm",)], writes=[("acc", c)])
                    S.op("dve", lambda e, c=c: e.tensor_tensor(out=acc[:, c, :], in0=acc[:, c, :], in1=lnr, op=ALU.mult),
                         reads=[("acc", c), ("lnr",)], writes=[("acc", c)])
                    S.op("act", lambda e, c=c: e.activation(out=mix[:, c, :], in_=acc[:, c, :], func=AF.Silu,
                                                            bias=pp[:, P0 + PP_LAB + c:P0 + PP_LAB + c + 1],
                                                            scale=pp[:, P0 + PP_LAG + c:P0 + PP_LAG + c + 1]),
                         reads=[("acc", c), ("pp",)], writes=[("mix", mb, c)])

            def m_outproj(t):
                mix = mix2[:, t % 2]
                mb = t % 2
                tok = slice(t * TILE, (t + 1) * TILE)
                wO = wring[:, unit_off(uO):unit_off(uO) + 8 * D].rearrange("p (k n) -> p k n", k=8)
                for dc in range(8):
                    b = bank()
                    for k in range(8):
                        S.op("pe", lambda e, k=k: e.matmul(ps[:, b, :], wO[:, k, dc * 128:(dc + 1) * 128], mix[:, k, :], start=(k == 0), stop=(k == 7)),
                             reads=[("w", uO), ("mix", mb, k)], writes=[("ps", b)], signal=(k == 7))
                    S.op("dve", lambda e: e.tensor_tensor(out=xT[:, dc, tok], in0=ps[:, b, :], in1=xT[:, dc, tok], op=ALU.add),
                         reads=[("ps", b), xreg(t, dc)], writes=[xreg(t, dc)])
                if t == NT - 1:
                    ring_release(uO)

            m_norm(0)
            m_inproj(0)
            for t in range(NT):
                bm, be = m_conv(t)
                if t + 1 < NT:
                    m_norm(t + 1)
                m_lnapply(t, bm, be)
                if t + 1 < NT:
                    m_inproj(t + 1)
                m_outproj(t)

        io_ctr = [0]

        def load_seq(s):
            for j in range(seq // 128):
                sl = io_ctr[0] % NSTG
                io_ctr[0] += 1
                S.dma("sp", [(stg[:, sl, :], x_d[s, j * 128:(j + 1) * 128, :])], "io%d" % (io_ctr[0] % 4), writes=[("stg", sl)])
                b = bank_pair()
                for c in range(8):
                    S.op("pe", lambda e, c=c: e.transpose(ps[:, b + c // 4, (c % 4) * 128:(c % 4 + 1) * 128], stg[:, sl, c * 128:(c + 1) * 128], ident[:]),
                         reads=[("stg", sl), ("ident",)], writes=[("ps", b + c // 4)])
                t = j // 4
                tk = slice(j * 128, (j + 1) * 128)
                S.op("act", lambda e: e.activation(out=xT[:, 0:4, tk], in_=ps[:, b, :].rearrange("p (c n) -> p c n", c=4), func=AF.Copy),
                     reads=[("ps", b)], writes=[xreg(t, c) for c in range(4)])
                S.op("dve", lambda e: e.tensor_copy(out=xT[:, 4:8, tk], in_=ps[:, b + 1, :].rearrange("p (c n) -> p c n", c=4)),
                     reads=[("ps", b + 1)], writes=[xreg(t, c) for c in range(4, 8)])

        def store_seq(s):
            gf = gfin
            if do_final:
                S.dma("sp", [(gfin, bc_d[:, L * 512:L * 512 + D])], "c0", writes=[("gfin",)])
            for j in range(seq // 128):
                sl = io_ctr[0] % NSTG
                io_ctr[0] += 1
                t = j // 4
                tk = slice(j * 128, (j + 1) * 128)
                b = bank_pair()
                for c in range(8):
                    S.op("pe", lambda e, c=c: e.transpose(ps[:, b + c // 4, (c % 4) * 128:(c % 4 + 1) * 128], xT[:, c, tk], ident[:]),
                         reads=[xreg(t, c), ("ident",)], writes=[("ps", b + c // 4)])
                pv = ps[:, b:b + 2, :].rearrange("p a n -> p (a n)")
                if do_final:
                    S.op("act", lambda e: e.activation(out=junk, in_=pv, func=AF.Square, accum_out=small[:, 0:1]),
                         reads=[("ps", b), ("ps", b + 1)], writes=[("junk",), ("ssq",)])
                    S.op("act", lambda e: e.activation(out=small[:, 1:2], in_=small[:, 0:1], func=AF.Sqrt, bias=epsr[:, 0:1], scale=1.0 / D),
                         reads=[("ssq",), ("epsr",)], writes=[("fstd",)])
                    S.op("dve", lambda e: e.reciprocal(out=small[:, 2:3], in_=small[:, 1:2]), reads=[("fstd",)], writes=[("frstd",)])
                    S.op("dve", lambda e: e.scalar_tensor_tensor(out=stg[:, sl, :], in0=pv, scalar=small[:, 2:3], in1=gf, op0=ALU.mult, op1=ALU.mult),
                         reads=[("ps", b), ("ps", b + 1), ("frstd",), ("gfin",)], writes=[("stg", sl)])
                else:
                    S.op("act", lambda e: e.activation(out=stg[:, sl, :], in_=pv, func=AF.Copy),
                         reads=[("ps", b), ("ps", b + 1)], writes=[("stg", sl)])
                S.dma("sp", [(y_d[s, j * 128:(j + 1) * 128, :], stg[:, sl, :])], "io%d" % (io_ctr[0] % 4), reads=[("stg", sl)])

        S.begin("all")
        for (s, l, f1, mu, f2) in plan:
            if l == 0:
                load_seq(s)
            ffn_phase(l, PP_G1, f1)
            mixer_phase(l, mu)
            ffn_phase(l, PP_G2, f2)
            if l == L - 1:
                store_seq(s)
        S.flush()
        for k in ("io0", "io1", "io2", "io3"):
            if dsems[k][1] > 0:
                nc.sync.wait_ge(dsems[k][0], dsems[k][1])
    return nc


def host_layout(inputs, L):
    f = lambda k: np.asarray(inputs[k], dtype=np.float32)
    pp = np.zeros((128, L * PPL), np.float32)
    bc = np.zeros((128, L * 512 + 1024), np.float32)
    wsT = np.zeros((128, L * 512), np.float32)
    for l in range(L):
        o = l * PPL
        pp[:, o + PP_G1:o + PP_G1 + 8] = f("ffn1_norm")[l].reshape(8, 128).T
        pp[:, o + PP_GM:o + PP_GM + 8] = f("mix_norm")[l].reshape(8, 128).T
        pp[:, o + PP_G2:o + PP_G2 + 8] = f("ffn2_norm")[l].reshape(8, 128).T
        caw = f("conv_a_w")[l]
        pp[:, o + PP_CAW:o + PP_CAW + 93] = caw.reshape(31, 3, 128).transpose(2, 1, 0).reshape(128, 93)
        pp[:, o + PP_CAB:o + PP_CAB + 3] = f("conv_a_b")[l].reshape(3, 128).T
        pp[:, o + PP_LAG:o + PP_LAG + 3] = f("ln_a_g")[l].reshape(3, 128).T
        pp[:, o + PP_LAB:o + PP_LAB + 3] = f("ln_a_b")[l].reshape(3, 128).T
        ccw = f("conv_c_w")[l]
        pp[:, o + PP_CCW:o + PP_CCW + 9] = ccw.reshape(3, 3, 128).transpose(2, 1, 0).reshape(128, 9)
        pp[:, o + PP_BS:o + PP_BS + 4] = f("b_s")[l].T
        bc[:, l * 512:l * 512 + 256] = f("ln_v_g")[l][None, :]
        bc[:, l * 512 + 256:l * 512 + 512] = f("ln_v_b")[l][None, :]
        wsT[:, l * 512:(l + 1) * 512] = f("w_s")[l].transpose(2, 0, 1).reshape(128, 512)
    bc[:, L * 512:] = f("final_norm")[None, :]
    cst = np.zeros((128, 640), np.float32)
    cst[:, 0:128] = np.eye(128, dtype=np.float32)
    tri = (np.arange(128)[:, None] <= np.arange(128)[None, :]).astype(np.float32)
    cst[:, 128:640] = np.tile(tri, (1, 4))
    return pp, bc, cst, wsT


_CACHE = {}

WNAMES = ["ffn1_w_gate", "ffn1_w_up", "ffn1_w_down", "w_in", "w_out", "ffn2_w_gate", "ffn2_w_up", "ffn2_w_down"]


def kernel(**inputs):
    x = np.ascontiguousarray(np.asarray(inputs["x"], dtype=np.float32))
    B, SEQ, _ = x.shape
    L = int(np.asarray(inputs["ffn1_norm"]).shape[0])
    ncores = 8
    nseq = B // ncores
    key = (nseq, SEQ, L)
    if key not in _CACHE:
        _CACHE[key] = build_program(nseq, SEQ, L)
    nc = _CACHE[key]
    pp, bc, cst, wsT = host_layout(inputs, L)
    shared = {k: np.ascontiguousarray(np.asarray(inputs[k], dtype=np.float32)) for k in WNAMES}
    shared.update(pp=pp, bc=bc, cst=cst, wsT=wsT)
    in_maps = []
    for c in range(ncores):
        m = dict(shared)
        m["x"] = x[c * nseq:(c + 1) * nseq]
        in_maps.append(m)
    res = run_bass_kernel_spmd(nc, in_maps, core_ids=list(range(ncores)))
    return np.concatenate([r["y"] for r in res.results], axis=0)
```

```python
import numpy as np
from contextlib import ExitStack

import concourse.bass as bass
import concourse.mybir as mybir
from concourse.bass_utils import run_bass_kernel_spmd

F32 = mybir.dt.float32
BF16 = mybir.dt.bfloat16
AF = mybir.ActivationFunctionType
ALU = mybir.AluOpType

D = 1024
DFF = 2816
NFC = DFF // 128
DPROJ = 2432
TILE = 512
RMS_EPS = 1e-6
LN_EPS = 1e-5
FFN_GROUPS = [4, 4, 4, 4, 3, 3]
PPL = 139
RING = 28672
SCR = 16960

PP_G1, PP_GM, PP_G2 = 0, 8, 16
PP_CAW, PP_CAB, PP_LAG, PP_LAB, PP_CCW, PP_BS = 24, 117, 120, 123, 126, 135


GRAN = 256
_ESZ = {}


def _phys(ap):
    try:
        if ap.tensor.name != "sb_scr":
            return ()
    except Exception:
        return ()
    esz = 2 if ap.dtype == BF16 else 4
    dims = list(ap.ap)
    pstep = int(dims[0][0]) if int(dims[0][0]) > 0 else 1 << 30
    off = int(ap.offset) % pstep
    ext = 1
    for st, cnt in dims[1:]:
        ext += (int(cnt) - 1) * abs(int(st))
    lo = off * esz
    hi = (off + ext) * esz
    return tuple(("g", i) for i in range(lo // GRAN, (hi - 1) // GRAN + 1))


def _is_ap(v):
    return hasattr(v, "tensor") and hasattr(v, "ap") and hasattr(v, "offset")


def _call_phys(name, a, k):
    pr, pw = [], []
    outs = []
    if "out" in k:
        outs.append(k["out"])
    elif a:
        outs.append(a[0])
    if k.get("accum_out") is not None:
        outs.append(k["accum_out"])
    for v in outs:
        if _is_ap(v):
            pw.extend(_phys(v))
    for i, v in enumerate(a):
        if i == 0 and "out" not in k:
            continue
        if _is_ap(v):
            pr.extend(_phys(v))
    for key, v in k.items():
        if key in ("out", "accum_out"):
            continue
        if _is_ap(v):
            pr.extend(_phys(v))
    return pr, pw


class _Rec:
    def __init__(self):
        self.call = None

    def __getattr__(self, name):
        def f(*a, **k):
            self.call = (name, a, k)
            return self
        return f


def _free_size(ap):
    n = 1
    for d in list(ap.shape)[1:]:
        n *= int(d)
    return n


def _est_dur(eng, name, a, k):
    out = k.get("out", a[0] if a else None)
    try:
        n = _free_size(out)
    except Exception:
        n = 512
    if eng == "pe":
        if name == "transpose":
            return 70.0
        rhs = k.get("rhs", a[2] if len(a) > 2 else None)
        try:
            n = _free_size(rhs)
        except Exception:
            pass
        return max(n, 64) * 0.42
    if eng == "act":
        return 230.0 + 0.75 * n
    if name == "reciprocal":
        return 60.0 + 2.7 * n
    return 60.0 + 1.2 * n


class Sched:
    def __init__(self, nc, sems, dsems):
        self.nc = nc
        self.e = {"pe": nc.tensor, "act": nc.scalar, "dve": nc.vector, "pool": nc.gpsimd, "sp": nc.sync}
        self.sem = sems
        self.dsem = dsems
        self.count = {k: 0 for k in ("pe", "act", "dve", "pool")}
        self.seen = {}
        self.lw = {}
        self.rd = {}
        self.win = None

    def _semobj(self, key):
        return self.sem[key] if key in self.sem else self.dsem[key][0]

    def _need(self, waiter, tok, waits):
        kind, key, idx = tok
        if kind == "e" and key != waiter:
            assert idx <= self.count[key], "wait on a pending (unsignaled) %s instruction from %s" % (key, waiter)
        if kind == "e" and key == waiter:
            if waiter == "pe":
                return
            if idx <= self.count[key] - 2:
                return
        if self.seen.get((waiter, key), 0) >= idx:
            return
        if waits.get(key, 0) < idx:
            waits[key] = idx

    def _collect(self, waiter, reads, writes):
        waits = {}
        for r in reads:
            t = self.lw.get(r)
            if t is not None:
                self._need(waiter, t, waits)
        for r in writes:
            t = self.lw.get(r)
            if t is not None:
                self._need(waiter, t, waits)
            for t in self.rd.get(r, {}).values():
                self._need(waiter, t, waits)
        return waits

    def _emit_waits(self, waiter, waits):
        E = self.e[waiter]
        for key, val in waits.items():
            E.wait_ge(self._semobj(key), val)
            self.seen[(waiter, key)] = val

    def _record(self, tok, reads, writes):
        for r in reads:
            self.rd.setdefault(r, {})[tok[1]] = tok
        for r in writes:
            self.lw[r] = tok
            self.rd[r] = {}

    def _op_now(self, eng, fn, reads=(), writes=(), signal=True):
        waits = self._collect(eng, reads, writes)
        self._emit_waits(eng, waits)
        ins = fn(self.e[eng])
        if signal:
            self.count[eng] += 1
            ins.then_inc(self.sem[eng], 1)
            self._record(("e", eng, self.count[eng]), reads, writes)
        else:
            self._record(("e", eng, self.count[eng] + 1), reads, writes)

    def _dma_now(self, q, pairs, semname, reads=(), writes=(), **kw):
        waits = self._collect(q, reads, writes)
        ds = self.dsem[semname]
        if ds[1] > self.seen.get((q, semname), 0):
            waits[semname] = max(waits.get(semname, 0), ds[1])
        self._emit_waits(q, waits)
        E = self.e[q]
        for out, in_ in pairs:
            E.dma_start(out=out, in_=in_, **kw).then_inc(ds[0], 16)
            ds[1] += 16
        self._record(("d", semname, ds[1]), reads, writes)

    def begin(self, tag=""):
        assert self.win is None
        import os
        if tag not in os.environ.get("KDEFER", "all").split(","):
            return
        self.win = []
        self.open_grp = {}

    def op(self, eng, fn, reads=(), writes=(), signal=True):
        rec = _Rec()
        fn(rec)
        name, a, k = rec.call
        pr, pw = _call_phys(name, a, k)
        reads = tuple(reads) + tuple(pr)
        writes = tuple(writes) + tuple(pw)
        if self.win is None:
            return self._op_now(eng, lambda e: getattr(e, name)(*a, **k), reads, writes, signal)
        dur = _est_dur(eng, name, a, k)
        g = self.open_grp.get(eng)
        if g is None:
            g = dict(kind="op", eng=eng, ins=[], reads=set(), writes=set(), dur=0.0)
            self.win.append(g)
            if not signal:
                self.open_grp[eng] = g
        g["ins"].append((name, a, k, reads, writes, signal))
        g["reads"].update(reads)
        g["writes"].update(writes)
        g["dur"] += dur
        if signal and eng in self.open_grp:
            del self.open_grp[eng]

    def dma(self, q, pairs, semname, reads=(), writes=(), **kw):
        reads = list(reads)
        writes = list(writes)
        for out, in_ in pairs:
            writes.extend(_phys(out))
            reads.extend(_phys(in_))
        if self.win is None:
            return self._dma_now(q, pairs, semname, reads, writes, **kw)
        self.win.append(dict(kind="dma", eng=q, pairs=pairs, semname=semname, reads=set(reads), writes=set(writes),
                             rl=tuple(reads), wl=tuple(writes), kw=kw, dur=2000.0))

    def flush(self):
        if self.win is None:
            return
        nodes = self.win
        assert not self.open_grp, "open unsignaled group at flush"
        self.win = None
        n = len(nodes)
        lastw = {}
        rds = {}
        preds = [set() for _ in range(n)]
        for i, nd in enumerate(nodes):
            for r in nd["reads"]:
                w = lastw.get(r)
                if w is not None and w != i:
                    preds[i].add(w)
            for r in nd["writes"]:
                w = lastw.get(r)
                if w is not None and w != i:
                    preds[i].add(w)
                for j in rds.get(r, ()):
                    if j != i:
                        preds[i].add(j)
            for r in nd["reads"]:
                rds.setdefault(r, set()).add(i)
            for r in nd["writes"]:
                lastw[r] = i
                rds[r] = set()
        succs = [[] for _ in range(n)]
        npred = [len(p) for p in preds]
        for i, p in enumerate(preds):
            for j in p:
                succs[j].append(i)
        import os
        LAT = float(os.environ.get("KLAT", "600"))
        BUCKET = float(os.environ.get("KBUCKET", "400"))
        tail = [0.0] * n
        for i in range(n - 1, -1, -1):
            m = 0.0
            for j in succs[i]:
                v = tail[j] + (0.0 if nodes[j]["eng"] == nodes[i]["eng"] else LAT)
                if v > m:
                    m = v
            tail[i] = nodes[i]["dur"] + m
        efree = {}
        fin = [0.0] * n
        rtime = [0.0] * n
        ready = [i for i in range(n) if npred[i] == 0]
        order = []
        while ready:
            best = None
            bkey = None
            for i in ready:
                st = max(efree.get(nodes[i]["eng"], 0.0), rtime[i])
                key = (int(st / BUCKET), -tail[i], i)
                if bkey is None or key < bkey:
                    bkey = key
                    best = i
            ready.remove(best)
            nd = nodes[best]
            st = max(efree.get(nd["eng"], 0.0), rtime[best])
            if os.environ.get("KCRIT"):
                if not hasattr(self, "_bind"):
                    self._bind = {}
                    self._elast = {}
                    self._rsrc = {}
                if rtime[best] >= efree.get(nd["eng"], 0.0):
                    self._bind[best] = ("dep", self._rsrc.get(best))
                else:
                    self._bind[best] = ("eng", self._elast.get(nd["eng"]))
                self._elast[nd["eng"]] = best
            fin[best] = st + nd["dur"]
            efree[nd["eng"]] = fin[best] if nd["kind"] == "op" else st + 100.0
            order.append(best)
            for j in succs[best]:
                lat = 0.0 if nodes[j]["eng"] == nd["eng"] else LAT
                if fin[best] + lat > rtime[j]:
                    rtime[j] = fin[best] + lat
                    if os.environ.get("KCRIT"):
                        self._rsrc[j] = best
                npred[j] -= 1
                if npred[j] == 0:
                    ready.append(j)
        assert len(order) == n
        import os
        if os.environ.get("KSCHED", "1") == "0":
            order = list(range(n))
        self.est_time = getattr(self, "est_time", 0.0) + (max(fin) if n else 0.0)
        if os.environ.get("KCRIT"):
            tq = float(os.environ["KCRIT"]) * 1000.0
            cand = [i for i in range(n) if nodes[i]["eng"] == "pe" and fin[i] <= tq]
            cur = max(cand, key=lambda i: fin[i])
            for _ in range(int(os.environ.get("KCRITN", "60"))):
                nd = nodes[cur]
                nm = nd["ins"][0][0] if nd["kind"] == "op" else "dma"
                kind, prev = self._bind.get(cur, (None, None))
                w = sorted(str(r) for r in nd["writes"] if r[0] != "g")[:2]
                print("%8.1f-%8.1f %-4s %-22s n=%d W=%s  <- %s" % ((fin[cur] - nd["dur"]) / 1e3, fin[cur] / 1e3, nd["eng"], nm, len(nd.get("ins", [])), w, kind))
                if prev is None:
                    break
                cur = prev
        if os.environ.get("KVERB"):
            busy = {}
            for i in range(n):
                busy[nodes[i]["eng"]] = busy.get(nodes[i]["eng"], 0.0) + nodes[i]["dur"]
            B = 50000.0
            hist = {}
            for i in range(n):
                if nodes[i]["eng"] == "pe":
                    st_ = fin[i] - nodes[i]["dur"]
                    b0 = int(st_ // B)
                    hist[b0] = hist.get(b0, 0.0) + nodes[i]["dur"]
            print("PE util per 50us:", " ".join("%d" % round(100 * hist.get(b, 0.0) / B) for b in range(int(max(fin) // B) + 1)))
            print("sched: nodes", n, "est makespan us %.1f" % (max(fin) / 1e3), {k: round(v / 1e3, 1) for k, v in busy.items()})
        if os.environ.get("KDUMP"):
            for pos, i in enumerate(order[:int(os.environ["KDUMP"])]):
                nd = nodes[i]
                nm = nd["ins"][0][0] if nd["kind"] == "op" else "dma"
                print(pos, i, nd["eng"], nm, len(nd.get("ins", [])), "W", sorted(map(str, nd["writes"]))[:3], "R", sorted(map(str, nd["reads"]))[:4], "t=%.0f" % fin[i])
        for i in order:
            nd = nodes[i]
            if nd["kind"] == "dma":
                self._dma_now(nd["eng"], nd["pairs"], nd["semname"], nd["rl"], nd["wl"], **nd["kw"])
            else:
                for (name, a, k, rl, wl, sig) in nd["ins"]:
                    self._op_now(nd["eng"], lambda e: getattr(e, name)(*a, **k), rl, wl, sig)

    def barrier_q(self, q):
        assert self.win is None
        waits = {}
        for p in ("pe", "act", "dve"):
            if self.count[p] > self.seen.get((q, p), 0):
                waits[p] = self.count[p]
        self._emit_waits(q, waits)

    def barrier(self):
        assert self.win is None
        engs = ("pe", "act", "dve")
        tgt = dict(self.count)
        for e in engs:
            waits = {}
            for p in engs:
                if tgt[p] > self.seen.get((e, p), 0):
                    waits[p] = tgt[p]
            for k, ds in self.dsem.items():
                if k.startswith("io") and ds[1] > self.seen.get((e, k), 0):
                    waits[k] = ds[1]
            self._emit_waits(e, waits)


def build_program(nseq, seq, nlayers, do_final=True, ring=RING):
    NT = seq // TILE
    nc = bass.Bass("TRN2", target_bir_lowering=False)
    L = nlayers

    def din(name, shape):
        return nc.dram_tensor(name, list(shape), F32, kind="ExternalInput").ap()

    x_d = din("x", [nseq, seq, D])
    wd = {}
    for f in ("ffn1", "ffn2"):
        wd[f + "_w_gate"] = din(f + "_w_gate", [L, D, DFF])
        wd[f + "_w_up"] = din(f + "_w_up", [L, D, DFF])
        wd[f + "_w_down"] = din(f + "_w_down", [L, DFF, D])
    wd["w_in"] = din("w_in", [L, D, DPROJ])
    wd["w_out"] = din("w_out", [L, D, D])
    pp_d = din("pp", [128, L * PPL])
    bc_d = din("bc", [128, L * 512 + 1024])
    cst_d = din("cst", [128, 128 + 512])
    ws_d = din("wsT", [128, L * 512])
    y_d = nc.dram_tensor("y", [nseq, seq, D], F32, kind="ExternalOutput").ap()

    with ExitStack() as es:
        def sb(name, shape, dt):
            return es.enter_context(nc.sbuf_tensor("sb_" + name, list(shape), dt))

        xT = sb("xT", [128, 8, seq], F32)
        scr = sb("scr", [128, SCR], F32)
        wring = sb("wring", [128, ring], BF16)
        sq = sb("sq", [128, 6, TILE], BF16)
        stdb = sb("stdb", [128, 2, TILE], F32)
        sg = sb("sg", [128, 2, TILE], F32)
        ident = sb("ident", [128, 128], F32)
        ones_d = sb("ones_d", [128, 128], BF16)
        ones_a = sb("ones_a", [128, 128], BF16)
        identb = sb("identb", [128, 128], BF16)
        pp = sb("pp", [128, L * PPL], F32)
        bc = sb("bc", [128, 512], F32)
        wsT = sb("wsT", [128, L * 512], BF16)
        epsr = sb("epsr", [128, 1], F32)
        epsl = sb("epsl", [128, 1], F32)
        small = sb("small", [128, 32], F32)
        ps = es.enter_context(nc.psum_tensor("ps", [128, 8, TILE], F32))

        sems = {k: es.enter_context(nc.semaphore("s_" + k)) for k in ("pe", "act", "dve", "pool")}
        dnames = ["w%d" % i for i in range(8)] + ["io%d" % i for i in range(4)] + ["c0", "c1"]
        dsems = {k: [es.enter_context(nc.semaphore("d_" + k)), 0] for k in dnames}
        S = Sched(nc, sems, dsems)

        def carve_bf(off_words, nwords):
            return scr[:, off_words:off_words + nwords].bitcast(BF16)

        h_all = carve_bf(0, NT * 8 * TILE // 2).rearrange("p (t c n) -> p t c n", t=NT, c=8)
        act_b = carve_bf(11648, 2 * 4 * TILE // 2).rearrange("p (a f n) -> p a f n", a=2, f=4)
        hm2 = carve_bf(0, 4096).rearrange("p (b c n) -> p b c n", b=2, c=8)
        a_bf = carve_bf(4096, 816).rearrange("p (c n) -> p c n", c=3)
        m_buf = scr[:, 4928:4928 + 3 * 516].rearrange("p (c n) -> p c n", c=3)
        cv = scr[:, 6528:6528 + TILE].rearrange("p (c n) -> p c n", c=1)
        guv = scr[:, 7040:7040 + 2 * TILE].rearrange("p (c n) -> p c n", c=2)
        vn = scr[:, 8064:8064 + 512].rearrange("p (c n) -> p c n", c=2)
        vnb = carve_bf(8576, 256).rearrange("p (c n) -> p c n", c=2)
        btm = carve_bf(8832, 256).rearrange("p (c n) -> p c n", c=2)
        acc = scr[:, 9088:9088 + 3 * TILE].rearrange("p (c n) -> p c n", c=3)
        lnm = scr[:, 10624:10624 + TILE]
        lnr = scr[:, 11136:11136 + TILE]
        mix2 = carve_bf(11648, 4096).rearrange("p (b c n) -> p b c n", b=2, c=8)
        NDG = 16
        diag = carve_bf(15744, 1024).rearrange("p (c n) -> p c n", c=NDG)
        bnst = scr[:, 16768:16784].rearrange("p (c n) -> p c n", c=2)
        mvb = scr[:, 16832:16840].rearrange("p (c n) -> p c n", c=2)
        assert SCR >= 16896
        junk = scr[:, 0:D]
        gfin = scr[:, D:2 * D]
        wstage = scr[:, 2 * D:2 * D + L * 512]
        mask4 = scr[:, 2 * D + L * 512:2 * D + L * 512 + 512]
        NSTG = 8
        stg = scr[:, 4096:4096 + NSTG * D].rearrange("p (a n) -> p a n", a=NSTG)

        bank_ctr = [0]
        reserved = set()

        def bank():
            while True:
                b = bank_ctr[0] % 8
                bank_ctr[0] += 1
                if b not in reserved:
                    return b

        def bank_pair():
            if bank_ctr[0] % 2:
                bank_ctr[0] += 1
            b = bank_ctr[0] % 8
            bank_ctr[0] += 2
            return b

        units = []
        ring_state = {"head": 0, "live": [], "next": 0, "alloc": {}, "dead": []}

        def ring_try_prefetch():
            st = ring_state
            while st["next"] < len(units):
                u = units[st["next"]]
                size = u["size"]
                off = u["off"]
                assert off + size <= ring
                ok = True
                for (uid, lo, hi) in st["live"]:
                    if lo < off + size and off < hi:
                        ok = False
                        break
                if not ok:
                    return
                wr = [("w", u["uid"])]
                keep = []
                for (uid, lo, hi) in st["dead"]:
                    if lo < off + size and off < hi:
                        if ("w", uid) not in wr:
                            wr.append(("w", uid))
                        if lo < off:
                            keep.append((uid, lo, off))
                        if hi > off + size:
                            keep.append((uid, off + size, hi))
                    else:
                        keep.append((uid, lo, hi))
                st["dead"] = keep
                pairs = [(mk(off), src) for (mk, src) in u["dmas"]]
                S.dma("pool", pairs, "w%d" % (u["uid"] % 8), reads=(), writes=wr)
                st["alloc"][u["uid"]] = off
                st["live"].append((u["uid"], off, off + size))
                st["head"] = off + size
                st["next"] += 1

        def ring_release(uid):
            st = ring_state
            for i, (u, lo, hi) in enumerate(st["live"]):
                if u == uid:
                    st["dead"].append(st["live"].pop(i))
                    break
            ring_try_prefetch()

        def unit_off(uid):
            assert uid in ring_state["alloc"], "unit %d not loaded" % uid
            return ring_state["alloc"][uid]

        def add_ffn_units(pref, l):
            ids = []
            f0 = 0
            for G in FFN_GROUPS:
                uid = len(units)
                gcols = G * 128

                def mk_gu(which, G=G, gcols=gcols):
                    def mk(off):
                        v = wring[:, off:off + 8 * 2 * gcols].rearrange("p (k w n) -> p k w n", k=8, w=2)
                        return v[:, :, which, :]
                    return mk

                def mk_dn(G=G, gcols=gcols):
                    def mk(off):
                        o2 = off + 8 * 2 * gcols
                        return wring[:, o2:o2 + G * D].rearrange("p (f n) -> p f n", f=G)
                    return mk

                src_g = wd[pref + "_w_gate"][l, :, f0 * 128:(f0 + G) * 128].rearrange("(k p) n -> p k n", p=128)
                src_u = wd[pref + "_w_up"][l, :, f0 * 128:(f0 + G) * 128].rearrange("(k p) n -> p k n", p=128)
                src_d = wd[pref + "_w_down"][l, f0 * 128:(f0 + G) * 128, :].rearrange("(f p) n -> p f n", p=128)
                units.append(dict(uid=uid, size=8 * 2 * gcols + G * D, off=(12288 if len(ids) % 2 == 0 else 0),
                                  dmas=[(mk_gu(0), src_g), (mk_gu(1), src_u), (mk_dn(), src_d)]))
                ids.append((uid, G, f0))
                f0 += G
            return ids

        def add_mixer_units(l):
            ids = {}
            for name, c0, c1 in (("A", 0, 768), ("B", 768, 1280), ("C", 1280, 2432)):
                uid = len(units)
                ncol = c1 - c0

                def mk(off, ncol=ncol):
                    return wring[:, off:off + 8 * ncol].rearrange("p (k n) -> p k n", k=8)

                src = wd["w_in"][l, :, c0:c1].rearrange("(k p) n -> p k n", p=128)
                units.append(dict(uid=uid, size=8 * ncol, off={"A": 9216, "B": 15360, "C": 19456}[name], dmas=[(mk, src)]))
                ids[name] = (uid, ncol)
            uid = len(units)

            def mko(off):
                return wring[:, off:off + 8 * D].rearrange("p (k n) -> p k n", k=8)

            src = wd["w_out"][l].rearrange("(k p) n -> p k n", p=128)
            units.append(dict(uid=uid, size=8 * D, off=0, dmas=[(mko, src)]))
            ids["O"] = (uid, D)
            return ids

        plan = []
        for s in range(nseq):
            for l in range(L):
                plan.append((s, l, add_ffn_units("ffn1", l), add_mixer_units(l), add_ffn_units("ffn2", l)))

        S.dma("sp", [(pp[:], pp_d)], "c0", writes=[("pp",)])
        S.dma("sp", [(ident[:], cst_d[:, 0:128]), (mask4, cst_d[:, 128:640]), (wstage, ws_d)], "c0",
              writes=[("ident",), ("mask4",), ("wstage",)])
        S.op("dve", lambda e: e.memset(ones_d[:], 1.0 / 1024.0), writes=[("ones_d",)])
        S.op("dve", lambda e: e.memset(ones_a[:], 1.0 / 384.0), writes=[("ones_a",)])
        S.op("dve", lambda e: e.memset(epsr[:], RMS_EPS), writes=[("epsr",)])
        S.op("dve", lambda e: e.memset(epsl[:], LN_EPS), writes=[("epsl",)])
        S.op("dve", lambda e: e.tensor_copy(out=identb[:], in_=ident[:]), reads=[("ident",)], writes=[("identb",)])
        for l in range(L):
            S.op("dve", lambda e, l=l: e.tensor_tensor(out=wsT[:, l * 512:(l + 1) * 512], in0=wstage[:, l * 512:(l + 1) * 512],
                                                       in1=mask4, op=ALU.mult),
                 reads=[("wstage",), ("mask4",)], writes=[("wsT", l)])
        S.barrier()
        ring_try_prefetch()

        def xreg(t, c):
            return ("x", t, c)

        def rms_norm_tile(l, gcol, t, hdst, hreg):
            tok = slice(t * TILE, (t + 1) * TILE)
            b = bank()
            for c in range(8):
                slot = 3 + c % 3
                S.op("act", lambda e, c=c, slot=slot: e.activation(out=sq[:, slot, :], in_=xT[:, c, tok], func=AF.Square),
                     reads=[xreg(t, c)], writes=[("sq", slot)])
                S.op("pe", lambda e, c=c, slot=slot: e.matmul(ps[:, b, :], ones_d[:], sq[:, slot, :], start=(c == 0), stop=(c == 7)),
                     reads=[("sq", slot), ("ones_d",)], writes=[("ps", b)])
            S.op("act", lambda e: e.activation(out=stdb[:, 0, :], in_=ps[:, b, :], func=AF.Sqrt, bias=epsr[:, 0:1], scale=1.0),
                 reads=[("ps", b), ("epsr",)], writes=[("std",)])
            S.op("dve", lambda e: e.reciprocal(out=stdb[:, 1, :], in_=stdb[:, 0, :]), reads=[("std",)], writes=[("rstd",)])
            for c in range(8):
                S.op("dve", lambda e, c=c: e.scalar_tensor_tensor(out=hdst[:, c, :], in0=xT[:, c, tok],
                                                                  scalar=pp[:, l * PPL + gcol + c:l * PPL + gcol + c + 1],
                                                                  in1=stdb[:, 1, :], op0=ALU.mult, op1=ALU.mult),
                     reads=[xreg(t, c), ("rstd",), ("pp",)], writes=[hreg(c)])

        sg_ctr = [0]

        def ffn_phase(l, gcol, unit_ids):
            steps = [(gi, t) for gi in range(len(unit_ids)) for t in range(NT)]

            def GU(i):
                gi, t = steps[i]
                uid, G, f0 = unit_ids[gi]
                off = unit_off(uid)
                gu = wring[:, off:off + 8 * 2 * G * 128].rearrange("p (k w n) -> p k w n", k=8, w=2)
                ab = i % 2
                for fi in range(G):
                    bA = bank()
                    for k in range(8):
                        S.op("pe", lambda e, k=k: e.matmul(ps[:, bA, :], gu[:, k, 0, fi * 128:(fi + 1) * 128], h_all[:, t, k, :],
                                                           start=(k == 0), stop=(k == 7)),
                             reads=[("w", uid), ("h", t, k)], writes=[("ps", bA)], signal=(k == 7))
                    bB = bank()
                    for k in range(8):
                        S.op("pe", lambda e, k=k: e.matmul(ps[:, bB, :], gu[:, k, 1, fi * 128:(fi + 1) * 128], h_all[:, t, k, :],
                                                           start=(k == 0), stop=(k == 7)),
                             reads=[("w", uid), ("h", t, k)], writes=[("ps", bB)], signal=(k == 7))
                    sl = sg_ctr[0] % 2
                    sg_ctr[0] += 1
                    S.op("act", lambda e: e.activation(out=sg[:, sl, :], in_=ps[:, bA, :], func=AF.Silu),
                         reads=[("ps", bA)], writes=[("sg", sl)])
                    S.op("dve", lambda e: e.tensor_tensor(out=act_b[:, ab, fi, :], in0=sg[:, sl, :], in1=ps[:, bB, :], op=ALU.mult),
                         reads=[("sg", sl), ("ps", bB)], writes=[("act", ab, fi)])

            def DN(i):
                gi, t = steps[i]
                uid, G, f0 = unit_ids[gi]
                off = unit_off(uid)
                dn = wring[:, off + 8 * 2 * G * 128:off + 8 * 2 * G * 128 + G * D].rearrange("p (f n) -> p f n", f=G)
                ab = i % 2
                tok = slice(t * TILE, (t + 1) * TILE)
                for dc in range(8):
                    b = bank()
                    for fi in range(G):
                        S.op("pe", lambda e, fi=fi: e.matmul(ps[:, b, :], dn[:, fi, dc * 128:(dc + 1) * 128], act_b[:, ab, fi, :],
                                                             start=(fi == 0), stop=(fi == G - 1)),
                             reads=[("w", uid), ("act", ab, fi)], writes=[("ps", b)], signal=(fi == G - 1))
                    S.op("dve", lambda e: e.scalar_tensor_tensor(out=xT[:, dc, tok], in0=ps[:, b, :], scalar=0.5, in1=xT[:, dc, tok],
                                                                 op0=ALU.mult, op1=ALU.add),
                         reads=[("ps", b), xreg(t, dc)], writes=[xreg(t, dc)])
                if t == NT - 1:
                    ring_release(uid)

            for i in range(len(steps)):
                gi, t = steps[i]
                if gi == 0:
                    rms_norm_tile(l, gcol, t, h_all[:, t], lambda c, t=t: ("h", t, c))
                GU(i)
                if i > 0:
                    DN(i - 1)
            DN(len(steps) - 1)

        def mixer_phase(l, mu):
            uA, _ = mu["A"]
            uB, _ = mu["B"]
            uC, _ = mu["C"]
            uO, _ = mu["O"]
            P0 = l * PPL
            dg_ctr = [0]
            S.dma("sp", [(bc[:], bc_d[:, l * 512:(l + 1) * 512])], "c1", writes=[("bc",)])

            def m_norm(t):
                rms_norm_tile(l, PP_GM, t, hm2[:, t % 2], lambda c, t=t: ("hm", t % 2, c))

            def m_inproj(t):
                mix = mix2[:, t % 2]
                mb = t % 2
                hm = hm2[:, t % 2]
                wA = wring[:, unit_off(uA):unit_off(uA) + 8 * 768].rearrange("p (k n) -> p k n", k=8)
                wB = wring[:, unit_off(uB):unit_off(uB) + 8 * 512].rearrange("p (k n) -> p k n", k=8)
                wC = wring[:, unit_off(uC):unit_off(uC) + 8 * 1152].rearrange("p (k n) -> p k n", k=8)
                if t == 0:
                    S.op("dve", lambda e: e.memset(a_bf[:, :, 0:32], 0.0), writes=[("a_halo",)])
                    S.op("dve", lambda e: e.memset(m_buf[:, :, 0:4], 0.0), writes=[("m_halo",)])
                else:
                    S.op("act", lambda e: e.activation(out=a_bf[:, :, 0:32], in_=a_bf[:, :, 512:544], func=AF.Copy),
                         reads=[("a", 0), ("a", 1), ("a", 2)], writes=[("a_halo",)])
                    S.op("act", lambda e: e.activation(out=m_buf[:, :, 0:4], in_=m_buf[:, :, 512:516], func=AF.Copy),
                         reads=[("m", 0), ("m", 1), ("m", 2)], writes=[("m_halo",)])

                def proj(wv, c0, b, uid):
                    for k in range(8):
                        S.op("pe", lambda e, k=k: e.matmul(ps[:, b, :], wv[:, k, c0:c0 + 128], hm[:, k, :], start=(k == 0), stop=(k == 7)),
                             reads=[("w", uid), ("hm", mb, k)], writes=[("ps", b)], signal=(k == 7))

                for c in range(3):
                    bv = bank()
                    proj(wA, c * 128, bv, uA)
                    bg = bank()
                    proj(wA, 384 + c * 128, bg, uA)
                    sl = sg_ctr[0] % 2
                    sg_ctr[0] += 1
                    S.op("act", lambda e: e.activation(out=sg[:, sl, :], in_=ps[:, bg, :], func=AF.Sigmoid),
                         reads=[("ps", bg)], writes=[("sg", sl)])
                    S.op("dve", lambda e, c=c: e.tensor_tensor(out=a_bf[:, c, 32:544], in0=sg[:, sl, :], in1=ps[:, bv, :], op=ALU.mult),
                         reads=[("sg", sl), ("ps", bv)], writes=[("a", c)])
                for c in range(3):
                    bb_ = bank()
                    proj(wC, c * 128, bb_, uC)
                    bcc = bank()
                    proj(wC, 384 + c * 128, bcc, uC)
                    bx = bank()
                    proj(wC, 768 + c * 128, bx, uC)
                    sl = sg_ctr[0] % 2
                    sg_ctr[0] += 1
                    S.op("act", lambda e: e.activation(out=sg[:, sl, :], in_=ps[:, bx, :], func=AF.Copy),
                         reads=[("ps", bx)], writes=[("sg", sl)])
                    S.op("dve", lambda e, c=c: e.tensor_tensor(out=m_buf[:, c, 4:516], in0=sg[:, sl, :], in1=ps[:, bcc, :], op=ALU.mult),
                         reads=[("sg", sl), ("ps", bcc)], writes=[("m", c)])
                    cs = 0
                    wcol = P0 + PP_CCW + c * 3
                    S.op("dve", lambda e, c=c: e.tensor_scalar(out=cv[:, cs, :], in0=m_buf[:, c, 2:514], scalar1=pp[:, wcol:wcol + 1], scalar2=None,
                                                               op0=ALU.mult),
                         reads=[("m", c), ("m_halo",), ("pp",)], writes=[("cv", cs)])
                    for k in (1, 2):
                        S.op("dve", lambda e, c=c, k=k: e.scalar_tensor_tensor(out=cv[:, cs, :], in0=m_buf[:, c, 2 + k:514 + k],
                                                                               scalar=pp[:, wcol + k:wcol + k + 1], in1=cv[:, cs, :],
                                                                               op0=ALU.mult, op1=ALU.add),
                             reads=[("m", c), ("m_halo",), ("cv", cs), ("pp",)], writes=[("cv", cs)])
                    S.op("dve", lambda e, c=c: e.tensor_tensor(out=mix[:, 5 + c, :], in0=cv[:, cs, :], in1=ps[:, bb_, :], op=ALU.mult),
                         reads=[("cv", cs), ("ps", bb_)], writes=[("mix", mb, 5 + c)])
                bT = bank()
                reserved.add(bT)
                psb = ps[:, bT, :].bitcast(BF16)
                for j in range(4):
                    buv = bank()
                    for k in range(8):
                        S.op("pe", lambda e, k=k: e.matmul(ps[:, buv, :], hm[:, k, j * 128:(j + 1) * 128], wB[:, k, :], start=(k == 0), stop=(k == 7)),
                             reads=[("w", uB), ("hm", mb, k)], writes=[("ps", buv)], signal=(k == 7))
                    gs = j % 2
                    S.op("act", lambda e: e.activation(out=guv[:, gs, :], in_=ps[:, buv, :], func=AF.Gelu_apprx_tanh),
                         reads=[("ps", buv)], writes=[("guv", gs)])
                    S.op("dve", lambda e: e.bn_stats(out=bnst[:, gs, 0:6], in_=guv[:, gs, 256:512]), reads=[("guv", gs)], writes=[("bnst", gs)])
                    S.op("dve", lambda e: e.bn_aggr(out=mvb[:, gs, 0:2], in_=bnst[:, gs, 0:6]), reads=[("bnst", gs)], writes=[("mv", gs)])
                    S.op("act", lambda e: e.activation(out=mvb[:, gs, 2:3], in_=mvb[:, gs, 1:2], func=AF.Sqrt, bias=epsl[:, 0:1], scale=1.0),
                         reads=[("mv", gs), ("epsl",)], writes=[("sdv", gs)])
                    S.op("dve", lambda e: e.reciprocal(out=mvb[:, gs, 3:4], in_=mvb[:, gs, 2:3]), reads=[("sdv", gs)], writes=[("rsv", gs)])
                    S.op("dve", lambda e: e.tensor_scalar(out=vn[:, gs, :], in0=guv[:, gs, 256:512], scalar1=mvb[:, gs, 0:1], scalar2=mvb[:, gs, 3:4],
                                                          op0=ALU.subtract, op1=ALU.mult),
                         reads=[("guv", gs), ("mv", gs), ("rsv", gs)], writes=[("vn", gs)])
                    S.op("dve", lambda e: e.tensor_tensor(out=vn[:, gs, :], in0=vn[:, gs, :], in1=bc[:, 0:256], op=ALU.mult),
                         reads=[("vn", gs), ("bc",)], writes=[("vn", gs)])
                    S.op("dve", lambda e: e.tensor_tensor(out=vnb[:, gs, :], in0=vn[:, gs, :], in1=bc[:, 256:512], op=ALU.add),
                         reads=[("vn", gs), ("bc",)], writes=[("vnb", gs)])
                    bs_ = bank()
                    for hh in range(4):
                        S.op("pe", lambda e, hh=hh: e.matmul(ps[:, bs_, hh * 64:(hh + 1) * 64], wsT[:, l * 512 + hh * 128:l * 512 + (hh + 1) * 128],
                                                             vnb[:, gs, hh * 64:(hh + 1) * 64], start=True, stop=True),
                             reads=[("wsT", l), ("vnb", gs)], writes=[("ps", bs_)])
                    for hh in range(4):
                        S.op("dve", lambda e, hh=hh: e.scalar_tensor_tensor(out=btm[:, gs, hh * 64:(hh + 1) * 64], in0=ps[:, bs_, hh * 64:(hh + 1) * 64],
                                                                            scalar=pp[:, P0 + PP_BS + hh:P0 + PP_BS + hh + 1],
                                                                            in1=guv[:, gs, hh * 64:(hh + 1) * 64], op0=ALU.add, op1=ALU.mult),
                             reads=[("ps", bs_), ("guv", gs), ("pp",)], writes=[("btm", gs)])
                    for cc in range(2):
                        S.op("pe", lambda e, cc=cc: e.transpose(psb[:, cc * 512 + j * 128:cc * 512 + (j + 1) * 128], btm[:, gs, cc * 128:(cc + 1) * 128], identb[:]),
                             reads=[("btm", gs), ("identb",)], writes=[("ps", bT)])
                for cc in range(2):
                    S.op("act", lambda e, cc=cc: e.activation(out=mix[:, 3 + cc, :], in_=psb[:, cc * 512:(cc + 1) * 512], func=AF.Copy),
                         reads=[("ps", bT)], writes=[("mix", mb, 3 + cc)])
                reserved.discard(bT)
                if t == NT - 1:
                    ring_release(uA)
                    ring_release(uB)
                    ring_release(uC)

            lslot = [0]

            def m_conv(t):
                lna = []
                for c in range(3):
                    b = bank()
                    for k in range(31):
                        ds_ = dg_ctr[0] % NDG
                        dg_ctr[0] += 1
                        wcol = P0 + PP_CAW + c * 31 + k
                        S.op("dve", lambda e: e.tensor_scalar(out=diag[:, ds_, :], in0=identb[:], scalar1=pp[:, wcol:wcol + 1], scalar2=None, op0=ALU.mult),
                             reads=[("identb",), ("pp",)], writes=[("diag", ds_)])
                        S.op("pe", lambda e: e.matmul(ps[:, b, :], diag[:, ds_, :], a_bf[:, c, 2 + k:514 + k], start=(k == 0), stop=(k == 30)),
                             reads=[("diag", ds_), ("a", c), ("a_halo",)], writes=[("ps", b)])
                    bcol = P0 + PP_CAB + c
                    S.op("dve", lambda e: e.tensor_scalar(out=acc[:, c, :], in0=ps[:, b, :], scalar1=pp[:, bcol:bcol + 1], scalar2=None, op0=ALU.add),
                         reads=[("ps", b), ("pp",)], writes=[("acc", c)])
                    lna.append(c)
                bm = bank()
                be = bank()
                for c in lna:
                    s0 = lslot[0] % 3
                    lslot[0] += 1
                    S.op("act", lambda e: e.activation(out=sq[:, s0, :], in_=acc[:, c, :], func=AF.Copy), reads=[("acc", c)], writes=[("sq", s0)])
                    S.op("pe", lambda e: e.matmul(ps[:, bm, :], ones_a[:], sq[:, s0, :], start=(c == 0), stop=(c == 2)),
                         reads=[("sq", s0), ("ones_a",)], writes=[("ps", bm)])
                    s1 = lslot[0] % 3
                    lslot[0] += 1
                    S.op("act", lambda e: e.activation(out=sq[:, s1, :], in_=acc[:, c, :], func=AF.Square), reads=[("acc", c)], writes=[("sq", s1)])
                    S.op("pe", lambda e: e.matmul(ps[:, be, :], ones_a[:], sq[:, s1, :], start=(c == 0), stop=(c == 2)),
                         reads=[("sq", s1), ("ones_a",)], writes=[("ps", be)])
                return bm, be

            def m_lnapply(t, bm, be):
                mix = mix2[:, t % 2]
                mb = t % 2
                S.op("act", lambda e: e.activation(out=lnm, in_=ps[:, bm, :], func=AF.Copy), reads=[("ps", bm)], writes=[("lnm",)])
                S.op("dve", lambda e: e.tensor_tensor(out=lnr, in0=lnm, in1=lnm, op=ALU.mult), reads=[("lnm",)], writes=[("lnr",)])
                S.op("dve", lambda e: e.tensor_tensor(out=lnr, in0=ps[:, be, :], in1=lnr, op=ALU.subtract),
                     reads=[("ps", be), ("lnr",)], writes=[("lnr",)])
                S.op("dve", lambda e: e.tensor_scalar(out=lnr, in0=lnr, scalar1=0.0, scalar2=None, op0=ALU.max),
                     reads=[("lnr",)], writes=[("lnr",)])
                S.op("act", lambda e: e.activation(out=lnr, in_=lnr, func=AF.Sqrt, bias=epsl[:, 0:1], scale=1.0),
                     reads=[("lnr",), ("epsl",)], writes=[("lnr",)])
                S.op("dve", lambda e: e.reciprocal(out=lnr, in_=lnr), reads=[("lnr",)], writes=[("lnr",)])
                for c in range(3):
                    S.op("dve", lambda e, c=c: e.tensor_tensor(out=acc[:, c, :], in0=acc[:, c, :], in1=lnm, op=ALU.subtract),
                         reads=[("acc", c), ("lnm",)], writes=[("acc", c)])
                    S.op("dve", lambda e, c=c: e.tensor_tensor(out=acc[:, c, :], in0=acc[:, c, :], in1=lnr, op=ALU.mult),
                         reads=[("acc", c), ("lnr",)], writes=[("acc", c)])
                    S.op("act", lambda e, c=c: e.activation(out=mix[:, c, :], in_=acc[:, c, :], func=AF.Silu,
                                                            bias=pp[:, P0 + PP_LAB + c:P0 + PP_LAB + c + 1],
                                                            scale=pp[:, P0 + PP_LAG + c:P0 + PP_LAG + c + 1]),
                         reads=[("acc", c), ("pp",)], writes=[("mix", mb, c)])

            def m_outproj(t):
                mix = mix2[:, t % 2]
                mb = t % 2
                tok = slice(t * TILE, (t + 1) * TILE)
                wO = wring[:, unit_off(uO):unit_off(uO) + 8 * D].rearrange("p (k n) -> p k n", k=8)
                for dc in range(8):
                    b = bank()
                    for k in range(8):
                        S.op("pe", lambda e, k=k: e.matmul(ps[:, b, :], wO[:, k, dc * 128:(dc + 1) * 128], mix[:, k, :], start=(k == 0), stop=(k == 7)),
                             reads=[("w", uO), ("mix", mb, k)], writes=[("ps", b)], signal=(k == 7))
                    S.op("dve", lambda e: e.tensor_tensor(out=xT[:, dc, tok], in0=ps[:, b, :], in1=xT[:, dc, tok], op=ALU.add),
                         reads=[("ps", b), xreg(t, dc)], writes=[xreg(t, dc)])
                if t == NT - 1:
                    ring_release(uO)

            m_norm(0)
            m_inproj(0)
            for t in range(NT):
                bm, be = m_conv(t)
                if t + 1 < NT:
                    m_norm(t + 1)
                m_lnapply(t, bm, be)
                if t + 1 < NT:
                    m_inproj(t + 1)
                m_outproj(t)

        io_ctr = [0]

        def load_seq(s):
            for j in range(seq // 128):
                sl = io_ctr[0] % NSTG
                io_ctr[0] += 1
                S.dma("sp", [(stg[:, sl, :], x_d[s, j * 128:(j + 1) * 128, :])], "io%d" % (io_ctr[0] % 4), writes=[("stg", sl)])
                b = bank_pair()
                for c in range(8):
                    S.op("pe", lambda e, c=c: e.transpose(ps[:, b + c // 4, (c % 4) * 128:(c % 4 + 1) * 128], stg[:, sl, c * 128:(c + 1) * 128], ident[:]),
                         reads=[("stg", sl), ("ident",)], writes=[("ps", b + c // 4)])
                t = j // 4
                tk = slice(j * 128, (j + 1) * 128)
                S.op("act", lambda e: e.activation(out=xT[:, 0:4, tk], in_=ps[:, b, :].rearrange("p (c n) -> p c n", c=4), func=AF.Copy),
                     reads=[("ps", b)], writes=[xreg(t, c) for c in range(4)])
                S.op("dve", lambda e: e.tensor_copy(out=xT[:, 4:8, tk], in_=ps[:, b + 1, :].rearrange("p (c n) -> p c n", c=4)),
                     reads=[("ps", b + 1)], writes=[xreg(t, c) for c in range(4, 8)])

        def store_seq(s):
            gf = gfin
            if do_final:
                S.dma("sp", [(gfin, bc_d[:, L * 512:L * 512 + D])], "c0", writes=[("gfin",)])
            for j in range(seq // 128):
                sl = io_ctr[0] % NSTG
                io_ctr[0] += 1
                t = j // 4
                tk = slice(j * 128, (j + 1) * 128)
                b = bank_pair()
                for c in range(8):
                    S.op("pe", lambda e, c=c: e.transpose(ps[:, b + c // 4, (c % 4) * 128:(c % 4 + 1) * 128], xT[:, c, tk], ident[:]),
                         reads=[xreg(t, c), ("ident",)], writes=[("ps", b + c // 4)])
                pv = ps[:, b:b + 2, :].rearrange("p a n -> p (a n)")
                if do_final:
                    S.op("act", lambda e: e.activation(out=junk, in_=pv, func=AF.Square, accum_out=small[:, 0:1]),
                         reads=[("ps", b), ("ps", b + 1)], writes=[("junk",), ("ssq",)])
                    S.op("act", lambda e: e.activation(out=small[:, 1:2], in_=small[:, 0:1], func=AF.Sqrt, bias=epsr[:, 0:1], scale=1.0 / D),
                         reads=[("ssq",), ("epsr",)], writes=[("fstd",)])
                    S.op("dve", lambda e: e.reciprocal(out=small[:, 2:3], in_=small[:, 1:2]), reads=[("fstd",)], writes=[("frstd",)])
                    S.op("dve", lambda e: e.scalar_tensor_tensor(out=stg[:, sl, :], in0=pv, scalar=small[:, 2:3], in1=gf, op0=ALU.mult, op1=ALU.mult),
                         reads=[("ps", b), ("ps", b + 1), ("frstd",), ("gfin",)], writes=[("stg", sl)])
                else:
                    S.op("act", lambda e: e.activation(out=stg[:, sl, :], in_=pv, func=AF.Copy),
                         reads=[("ps", b), ("ps", b + 1)], writes=[("stg", sl)])
                S.dma("sp", [(y_d[s, j * 128:(j + 1) * 128, :], stg[:, sl, :])], "io%d" % (io_ctr[0] % 4), reads=[("stg", sl)])

        S.begin("all")
        for (s, l, f1, mu, f2) in plan:
            if l == 0:
                load_seq(s)
            ffn_phase(l, PP_G1, f1)
            mixer_phase(l, mu)
            ffn_phase(l, PP_G2, f2)
            if l == L - 1:
                store_seq(s)
        S.flush()
        for k in ("io0", "io1", "io2", "io3"):
            if dsems[k][1] > 0:
                nc.sync.wait_ge(dsems[k][0], dsems[k][1])
    return nc


def host_layout(inputs, L):
    f = lambda k: np.asarray(inputs[k], dtype=np.float32)
    pp = np.zeros((128, L * PPL), np.float32)
    bc = np.zeros((128, L * 512 + 1024), np.float32)
    wsT = np.zeros((128, L * 512), np.float32)
    for l in range(L):
        o = l * PPL
        pp[:, o + PP_G1:o + PP_G1 + 8] = f("ffn1_norm")[l].reshape(8, 128).T
        pp[:, o + PP_GM:o + PP_GM + 8] = f("mix_norm")[l].reshape(8, 128).T
        pp[:, o + PP_G2:o + PP_G2 + 8] = f("ffn2_norm")[l].reshape(8, 128).T
        caw = f("conv_a_w")[l]
        pp[:, o + PP_CAW:o + PP_CAW + 93] = caw.reshape(31, 3, 128).transpose(2, 1, 0).reshape(128, 93)
        pp[:, o + PP_CAB:o + PP_CAB + 3] = f("conv_a_b")[l].reshape(3, 128).T
        pp[:, o + PP_LAG:o + PP_LAG + 3] = f("ln_a_g")[l].reshape(3, 128).T
        pp[:, o + PP_LAB:o + PP_LAB + 3] = f("ln_a_b")[l].reshape(3, 128).T
        ccw = f("conv_c_w")[l]
        pp[:, o + PP_CCW:o + PP_CCW + 9] = ccw.reshape(3, 3, 128).transpose(2, 1, 0).reshape(128, 9)
        pp[:, o + PP_BS:o + PP_BS + 4] = f("b_s")[l].T
        bc[:, l * 512:l * 512 + 256] = f("ln_v_g")[l][None, :]
        bc[:, l * 512 + 256:l * 512 + 512] = f("ln_v_b")[l][None, :]
        wsT[:, l * 512:(l + 1) * 512] = f("w_s")[l].transpose(2, 0, 1).reshape(128, 512)
    bc[:, L * 512:] = f("final_norm")[None, :]
    cst = np.zeros((128, 640), np.float32)
    cst[:, 0:128] = np.eye(128, dtype=np.float32)
    tri = (np.arange(128)[:, None] <= np.arange(128)[None, :]).astype(np.float32)
    cst[:, 128:640] = np.tile(tri, (1, 4))
    return pp, bc, cst, wsT


_CACHE = {}

WNAMES = ["ffn1_w_gate", "ffn1_w_up", "ffn1_w_down", "w_in", "w_out", "ffn2_w_gate", "ffn2_w_up", "ffn2_w_down"]


def kernel(**inputs):
    x = np.ascontiguousarray(np.asarray(inputs["x"], dtype=np.float32))
    B, SEQ, _ = x.shape
    L = int(np.asarray(inputs["ffn1_norm"]).shape[0])
    ncores = 8
    nseq = B // ncores
    key = (nseq, SEQ, L)
    if key not in _CACHE:
        _CACHE[key] = build_program(nseq, SEQ, L)
    nc = _CACHE[key]
    pp, bc, cst, wsT = host_layout(inputs, L)
    shared = {k: np.ascontiguousarray(np.asarray(inputs[k], dtype=np.float32)) for k in WNAMES}
    shared.update(pp=pp, bc=bc, cst=cst, wsT=wsT)
    in_maps = []
    for c in range(ncores):
        m = dict(shared)
        m["x"] = x[c * nseq:(c + 1) * nseq]
        in_maps.append(m)
    res = run_bass_kernel_spmd(nc, in_maps, core_ids=list(range(ncores)))
    return np.concatenate([r["y"] for r in res.results], axis=0)
```
